# Optimizing a Trainium2 kernel written in Bass

```python
import math
import jax, jax.numpy as jnp
from jax import lax
import numpy as np

D_MODEL = 2048
BATCH = 16
SEQ = 256
DEPTH = 2
DEC_BATCH = 2
DEC_SEQ = 2048
PAST_LEN = 512

GRID_W = 64
HEAD_DIM = 128
CONV_DIM = 1024
CONV_K = 3
WIN_HEADS = 8
WIN_KV_HEADS = 2
WINDOW = 128
BLOCK = 128
DIFF_HEADS = 8
DIFF_QK_DIM = 64
DIFF_V_DIM = 128
D_FF = 4 * D_MODEL
N_BRANCH = 3
EPS = 1e-6
ROPE_BASE = 10000.0
NEG = -1e30
IN_SIZES = (CONV_DIM, CONV_DIM, CONV_DIM,
            WIN_HEADS * HEAD_DIM, WIN_KV_HEADS * HEAD_DIM, WIN_KV_HEADS * HEAD_DIM,
            DIFF_HEADS * 2 * DIFF_QK_DIM, DIFF_HEADS * 2 * DIFF_QK_DIM, DIFF_HEADS * DIFF_V_DIM,
            N_BRANCH * D_MODEL)
IN_TOTAL = sum(IN_SIZES)

kernel_name = "hybrid_conv_window_diff_dit_step"


def rmsnorm(x, g):
    xf = x.astype(jnp.float32)
    y = xf * lax.rsqrt(jnp.mean(xf * xf, axis=-1, keepdims=True) + EPS)
    return (y * g.astype(jnp.float32)).astype(x.dtype)


def rope1d(x, pos):
    half = x.shape[-1] // 2
    inv = ROPE_BASE ** (-jnp.arange(half, dtype=jnp.float32) / half)
    ang = pos.astype(jnp.float32)[:, None] * inv[None, :]
    shape = (1, x.shape[1]) + (1,) * (x.ndim - 3) + (half,)
    cos = jnp.cos(ang).reshape(shape).astype(x.dtype)
    sin = jnp.sin(ang).reshape(shape).astype(x.dtype)
    x1, x2 = x[..., :half], x[..., half:]
    return jnp.concatenate([x1 * cos - x2 * sin, x1 * sin + x2 * cos], axis=-1)


def rope2d(x):
    T = x.shape[1]
    n_rows = T // GRID_W
    rows = jnp.repeat(jnp.arange(n_rows, dtype=jnp.int32), GRID_W)
    cols = jnp.tile(jnp.arange(GRID_W, dtype=jnp.int32), n_rows)
    h = x.shape[-1] // 2
    return jnp.concatenate([rope1d(x[..., :h], rows), rope1d(x[..., h:], cols)], axis=-1)


def short_conv(u, w):
    T = u.shape[1]
    pad = CONV_K // 2
    up = jnp.pad(u, ((0, 0), (pad, pad), (0, 0)))
    out = up[:, 0:T] * w[0]
    for j in range(1, CONV_K):
        out = out + up[:, j:j + T] * w[j]
    return out


def query_blocks(fn, q):
    B, T = q.shape[:2]
    nb = T // BLOCK
    qb = jnp.moveaxis(q.reshape((B, nb, BLOCK) + q.shape[2:]), 1, 0)
    out = lax.map(fn, qb)
    out = jnp.moveaxis(out, 0, 1)
    return out.reshape((B, T) + out.shape[3:])


def ctx_window_attn(q, k, v, sink):
    B, S, Hq, d = q.shape
    Hkv = k.shape[2]
    G = Hq // Hkv
    scale = d ** -0.5

    def blk(qb):
        qg = qb.reshape(B, BLOCK, Hkv, G, d)
        s = jnp.einsum('bqhgd,bkhd->bhgqk', qg, k).astype(jnp.float32) * scale
        s_sink = jnp.broadcast_to(sink.astype(jnp.float32).reshape(1, Hkv, G, 1, 1), s.shape[:-1] + (1,))
        p = jax.nn.softmax(jnp.concatenate([s_sink, s], axis=-1), axis=-1).astype(v.dtype)
        o = jnp.einsum('bhgqk,bkhd->bqhgd', p[..., 1:], v)
        return o.reshape(B, BLOCK, Hq * d)

    return query_blocks(blk, q)


def latent_window_attn(q, k, v, kc, vc, sink):
    B, T, Hq, d = q.shape
    Hkv = k.shape[2]
    G = Hq // Hkv
    nb = T // BLOCK
    P = kc.shape[1]
    scale = d ** -0.5
    qg = q.reshape(B, nb, BLOCK, Hkv, G, d)

    def bands(a):
        ap = jnp.pad(a, ((0, 0), (BLOCK, BLOCK), (0, 0), (0, 0)))
        parts = [ap[:, o:o + T].reshape(B, nb, BLOCK, Hkv, d) for o in (0, BLOCK, 2 * BLOCK)]
        return jnp.concatenate(parts, axis=2)

    kb, vb = bands(k), bands(v)
    qi = jnp.arange(BLOCK)[:, None]
    kj = jnp.arange(3 * BLOCK)[None, :] - BLOCK
    rel_ok = jnp.abs(qi - kj) <= WINDOW
    kpos = jnp.arange(nb)[:, None] * BLOCK + kj
    in_range = (kpos >= 0) & (kpos < T)
    mask = rel_ok[None] & in_range[:, None, :]
    s_loc = jnp.einsum('bnqhgd,bnkhd->bnhgqk', qg, kb).astype(jnp.float32) * scale
    s_loc = jnp.where(mask[None, :, None, None], s_loc, NEG)
    s_ctx = jnp.einsum('bnqhgd,bphd->bnhgqp', qg, kc).astype(jnp.float32) * scale
    s_sink = jnp.broadcast_to(sink.astype(jnp.float32).reshape(1, 1, Hkv, G, 1, 1), s_ctx.shape[:-1] + (1,))
    p = jax.nn.softmax(jnp.concatenate([s_sink, s_ctx, s_loc], axis=-1), axis=-1).astype(v.dtype)
    o = (jnp.einsum('bnhgqp,bphd->bnqhgd', p[..., 1:1 + P], vc)
         + jnp.einsum('bnhgqk,bnkhd->bnqhgd', p[..., 1 + P:], vb))
    return o.reshape(B, T, Hq * d)


def diff_attn(q, k, v, lam, subln_g, lam_init):
    B, T, H = q.shape[:3]
    scale = q.shape[-1] ** -0.5

    def blk(qb):
        s = jnp.einsum('bqhmd,bkhmd->bhmqk', qb, k).astype(jnp.float32) * scale
        p = jax.nn.softmax(s, axis=-1)
        a = (p[:, :, 0] - lam * p[:, :, 1]).astype(v.dtype)
        return jnp.einsum('bhqk,bkhd->bqhd', a, v)

    o = query_blocks(blk, q)
    o = rmsnorm(o, subln_g) * (1.0 - lam_init)
    return o.reshape(B, T, H * DIFF_V_DIM)


def trunk_layer(x, mod, l, W, cache):
    B, T, _ = x.shape
    shift1, scale1, gate1, shift2, scale2, gate2 = jnp.split(mod, 6, axis=-1)
    h = rmsnorm(x, W['norm_mix'][l]) * (1.0 + scale1) + shift1
    u = h @ W['w_in'][l]
    idx = []
    acc = 0
    for s in IN_SIZES[:-1]:
        acc += s
        idx.append(acc)
    cb, cc, cx, wq, wk, wv, dq, dk, dv, gts = jnp.split(u, idx, axis=-1)

    y_a = cb * short_conv(cc * cx, W['conv_w'][l])

    q = wq.reshape(B, T, WIN_HEADS, HEAD_DIM)
    k = wk.reshape(B, T, WIN_KV_HEADS, HEAD_DIM)
    v = wv.reshape(B, T, WIN_KV_HEADS, HEAD_DIM)
    q_d = dq.reshape(B, T, DIFF_HEADS, 2, DIFF_QK_DIM)
    k_d = dk.reshape(B, T, DIFF_HEADS, 2, DIFF_QK_DIM)
    v_d = dv.reshape(B, T, DIFF_HEADS, DIFF_V_DIM)
    lam_init = 0.8 - 0.6 * math.exp(-0.3 * l)
    lam = (jnp.exp(jnp.sum(W['lambda_q1'][l].astype(jnp.float32) * W['lambda_k1'][l].astype(jnp.float32)))
           - jnp.exp(jnp.sum(W['lambda_q2'][l].astype(jnp.float32) * W['lambda_k2'][l].astype(jnp.float32)))
           + lam_init)
    sink = W['win_sink'][l]

    if cache is None:
        y_b = ctx_window_attn(q, k, v, sink)
        y_c = diff_attn(q_d, k_d, v_d, lam, W['diff_subln'][l], lam_init)
        state = (k, v, k_d.reshape(B, T, DIFF_HEADS, 2 * DIFF_QK_DIM), v_d)
    else:
        kc_w, vc_w, kc_d, vc_d = cache
        P = kc_d.shape[1]
        y_b = latent_window_attn(rope2d(q), rope2d(k), v, kc_w, vc_w, sink)
        k_all = jnp.concatenate([kc_d.reshape(B, P, DIFF_HEADS, 2, DIFF_QK_DIM), rope2d(k_d)], axis=1)
        v_all = jnp.concatenate([vc_d, v_d], axis=1)
        y_c = diff_attn(rope2d(q_d), k_all, v_all, lam, W['diff_subln'][l], lam_init)
        state = None

    g_a, g_b, g_c = jnp.split(jax.nn.sigmoid(gts), N_BRANCH, axis=-1)
    merged = (g_a * (y_a @ W['w_branch_conv'][l])
              + g_b * (y_b @ W['w_branch_win'][l])
              + g_c * (y_c @ W['w_branch_diff'][l]))
    x = x + gate1 * (merged @ W['w_out'][l])

    h2 = rmsnorm(x, W['norm_mlp'][l]) * (1.0 + scale2) + shift2
    x = x + gate2 * (jnp.square(jax.nn.relu(h2 @ W['w_mlp1'][l])) @ W['w_mlp2'][l])
    return x, state


def setup_inputs(seed: int = 0) -> dict:
    key = jax.random.key(seed)
    ks = iter(jax.random.split(key, 40))

    def nrm(shape, scale):
        return jax.random.normal(next(ks), shape, jnp.float32) * scale

    D = D_MODEL
    return {
        'x_prompt': nrm((BATCH, SEQ, D), 1.0),
        'x_sample': nrm((DEC_BATCH, DEC_SEQ, D), 1.0),
        'cache_win_k': nrm((DEC_BATCH, DEPTH, PAST_LEN, WIN_KV_HEADS, HEAD_DIM), 1.0),
        'cache_win_v': nrm((DEC_BATCH, DEPTH, PAST_LEN, WIN_KV_HEADS, HEAD_DIM), 1.0),
        'cache_diff_k': nrm((DEC_BATCH, DEPTH, PAST_LEN, DIFF_HEADS, 2 * DIFF_QK_DIM), 1.0),
        'cache_diff_v': nrm((DEC_BATCH, DEPTH, PAST_LEN, DIFF_HEADS, DIFF_V_DIM), 1.0),
        'c': nrm((DEC_BATCH, D), 1.0),
        'c_ctx': nrm((D,), 1.0),
        'w_ada': nrm((DEPTH, D, 6 * D), 0.5 * D ** -0.5),
        'b_ada': nrm((DEPTH, 6 * D), 0.01),
        'norm_mix': 1.0 + nrm((DEPTH, D), 0.02),
        'norm_mlp': 1.0 + nrm((DEPTH, D), 0.02),
        'w_in': nrm((DEPTH, D, IN_TOTAL), D ** -0.5),
        'conv_w': nrm((DEPTH, CONV_K, CONV_DIM), CONV_K ** -0.5),
        'win_sink': nrm((DEPTH, WIN_HEADS), 0.5),
        'lambda_q1': nrm((DEPTH, DIFF_QK_DIM), 0.1),
        'lambda_k1': nrm((DEPTH, DIFF_QK_DIM), 0.1),
        'lambda_q2': nrm((DEPTH, DIFF_QK_DIM), 0.1),
        'lambda_k2': nrm((DEPTH, DIFF_QK_DIM), 0.1),
        'diff_subln': 1.0 + nrm((DEPTH, DIFF_V_DIM), 0.02),
        'w_branch_conv': nrm((DEPTH, CONV_DIM, D), CONV_DIM ** -0.5),
        'w_branch_win': nrm((DEPTH, WIN_HEADS * HEAD_DIM, D), (WIN_HEADS * HEAD_DIM) ** -0.5),
        'w_branch_diff': nrm((DEPTH, DIFF_HEADS * DIFF_V_DIM, D), (DIFF_HEADS * DIFF_V_DIM) ** -0.5),
        'w_out': nrm((DEPTH, D, D), D ** -0.5),
        'w_mlp1': nrm((DEPTH, D, D_FF), D ** -0.5),
        'w_mlp2': nrm((DEPTH, D_FF, D), D_FF ** -0.5),
        'norm_final': 1.0 + nrm((D,), 0.02),
    }


def reference(x_prompt, x_sample, cache_win_k, cache_win_v, cache_diff_k, cache_diff_v, c, c_ctx,
              w_ada, b_ada, norm_mix, norm_mlp, w_in, conv_w, win_sink,
              lambda_q1, lambda_k1, lambda_q2, lambda_k2, diff_subln,
              w_branch_conv, w_branch_win, w_branch_diff, w_out, w_mlp1, w_mlp2, norm_final):
    W = {'norm_mix': norm_mix, 'norm_mlp': norm_mlp, 'w_in': w_in, 'conv_w': conv_w,
         'win_sink': win_sink, 'lambda_q1': lambda_q1, 'lambda_k1': lambda_k1,
         'lambda_q2': lambda_q2, 'lambda_k2': lambda_k2, 'diff_subln': diff_subln,
         'w_branch_conv': w_branch_conv, 'w_branch_win': w_branch_win, 'w_branch_diff': w_branch_diff,
         'w_out': w_out, 'w_mlp1': w_mlp1, 'w_mlp2': w_mlp2}
    silu_ctx = jax.nn.silu(c_ctx)
    silu_c = jax.nn.silu(c)

    xp = x_prompt
    kw_l, vw_l, kd_l, vd_l = [], [], [], []
    for l in range(DEPTH):
        mod_ctx = silu_ctx @ w_ada[l] + b_ada[l]
        xp, st = trunk_layer(xp, mod_ctx, l, W, None)
        kw_l.append(st[0]); vw_l.append(st[1]); kd_l.append(st[2]); vd_l.append(st[3])
    y_prompt = rmsnorm(xp, norm_final)
    new_win_k = jnp.stack(kw_l, axis=1)
    new_win_v = jnp.stack(vw_l, axis=1)
    new_diff_k = jnp.stack(kd_l, axis=1)
    new_diff_v = jnp.stack(vd_l, axis=1)

    xs = x_sample
    for l in range(DEPTH):
        mod_lat = (silu_c @ w_ada[l] + b_ada[l])[:, None, :]
        cache = (cache_win_k[:, l], cache_win_v[:, l], cache_diff_k[:, l], cache_diff_v[:, l])
        xs, _ = trunk_layer(xs, mod_lat, l, W, cache)
    y_sample = rmsnorm(xs, norm_final)

    return (y_prompt, y_sample, new_win_k, new_win_v, new_diff_k, new_diff_v)
```

```python
from contextlib import ExitStack
import math
import numpy as np
import ml_dtypes
import concourse.bass as bass
import concourse.mybir as mybir
from concourse.bass_utils import run_bass_kernel_spmd

F32 = mybir.dt.float32
BF16 = mybir.dt.bfloat16
ALU = mybir.AluOpType
AF = mybir.ActivationFunctionType


class Prog:
    ENG = ('pe', 'act', 'dve', 'pool', 'sp')

    def __init__(self, nc):
        self.nc = nc
        self.stack = ExitStack()
        self.ops = {e: [] for e in self.ENG}
        self.sem = {}
        self.cnt = {}
        self.waited = {}
        self.last_w = {}
        self.readers = {}
        self.out_tokens = {}
        for e in self.ENG:
            self._mksem('E_' + e)

    def _mksem(self, name):
        if name not in self.sem:
            self.sem[name] = self.stack.enter_context(self.nc.semaphore(name))
            self.cnt[name] = 0
        return name

    def sb(self, name, shape, dt):
        return self.stack.enter_context(self.nc.sbuf_tensor(name, list(shape), dt))

    def ps(self, name, shape, dt=F32):
        return self.stack.enter_context(self.nc.psum_tensor(name, list(shape), dt))

    def _collect(self, eng, reads, writes):
        deps = {}

        def add(tok):
            if tok is None:
                return
            s, v, te = tok
            if te == 'pe' and eng == 'pe':
                return
            if deps.get(s, 0) < v:
                deps[s] = v
        for r in reads:
            add(self.last_w.get(r))
        for w in writes:
            add(self.last_w.get(w))
            for s, (v, te) in self.readers.get(w, {}).items():
                if te == eng and te != 'dma':
                    continue
                add((s, v, te))
        waits = []
        for s, v in deps.items():
            if self.waited.get((eng, s), 0) < v:
                self.waited[(eng, s)] = v
                waits.append((s, v))
        return waits

    def _record(self, tok, reads, writes):
        s, v, te = tok
        for r in reads:
            d = self.readers.setdefault(r, {})
            if d.get(s, (0, te))[0] <= v:
                d[s] = (v, te)
        for w in writes:
            self.last_w[w] = tok
            self.readers[w] = {}

    @staticmethod
    def _excl(reads, writes):
        pr = [r for r in reads if isinstance(r, str) and r.startswith('ps')]
        if not pr:
            return list(reads), list(writes)
        return [r for r in reads if r not in pr], list(writes) + [r for r in pr if r not in writes]

    def op(self, eng, fn, reads=(), writes=()):
        reads, writes = self._excl(reads, writes)
        waits = self._collect(eng, reads, writes)
        s = 'E_' + eng
        self.cnt[s] += 1
        self.ops[eng].append((waits, fn, (s, 1)))
        self._record((s, self.cnt[s], eng), reads, writes)

    def mm(self, fns, reads=(), writes=()):
        reads, writes = self._excl(reads, writes)
        waits = self._collect('pe', reads, writes)
        s = 'E_pe'
        self.cnt[s] += 1
        n = len(fns)
        for i, fn in enumerate(fns):
            self.ops['pe'].append((waits if i == 0 else [], fn, (s, 1) if i == n - 1 else None))
        self._record((s, self.cnt[s], 'pe'), reads, writes)

    def dma(self, q, fns, reads=(), writes=(), semkey=None, out=False, inc=16):
        if semkey is None:
            semkey = 'OUT' if out else 'D_' + str(writes[0])
        self._mksem(semkey)
        waits = self._collect(q, reads, writes)
        for i, fn in enumerate(fns):
            self.ops[q].append((waits if i == 0 else [], fn, (semkey, inc)))
        self.cnt[semkey] += inc * len(fns)
        tok = (semkey, self.cnt[semkey], 'dma')
        self._record(tok, reads, [] if out else writes)
        if out:
            self.out_tokens[q] = tok

    def finish(self):
        if 'OUT' in self.sem:
            v = self.cnt['OUT']
            self.ops['sp'].append(([('OUT', v)], None, None))
        nc = self.nc
        engs = {'pe': 'tensor', 'act': 'scalar', 'dve': 'vector', 'pool': 'gpsimd', 'sp': 'sync'}
        sem = self.sem
        with nc.Block() as block:
            for ename, bname in engs.items():
                oplist = self.ops[ename]

                def body(e, oplist=oplist):
                    for waits, fn, inc in oplist:
                        for s, v in waits:
                            e.wait_ge(sem[s], v)
                        if fn is None:
                            continue
                        ins = fn(e)
                        if inc is not None:
                            ins.then_inc(sem[inc[0]], inc[1])
                getattr(block, bname)(body)
        self.stack.close()


D_MODEL = 2048
NKC = 16
T = 512
NL = 2
KSUB = 8
NWB = 3
EPS = 1e-6
CB, CC, CX, WQ, WK, WV, DQ, DK, DV, GA, GB, GC = 0, 1024, 2048, 3072, 4096, 4352, 4608, 5632, 6656, 7680, 9728, 11776
IN_TOTAL = 13824
P_CV, P_BA, P_NMIX, P_NMLP, P_NFIN, P_CW, P_SINK, P_LQ1, P_LK1, P_LQ2, P_LK2, P_SUB, P_SELL, P_SELR, NPAR = \
    0, 32, 224, 256, 288, 304, 352, 368, 496, 624, 752, 880, 882, 886, 890
NRW = 256
NRD = 2052
LAM_INIT = [0.8 - 0.6 * math.exp(-0.3 * l) for l in range(NL)]
GROUPS = [[0, 1, 2, 3], [4, 5, 6, 7]]
_GROUPS_OVERRIDE = []
_DEBUG_NOCC = []


def MM(out, lhsT, rhs, start, stop):
    return lambda e: e.matmul(out, lhsT=lhsT, rhs=rhs, start=start, stop=stop)


def ACT(out, in_, func, **kw):
    return lambda e: e.activation(out=out, in_=in_, func=func, **kw)


def TT(out, in0, in1, op):
    return lambda e: e.tensor_tensor(out=out, in0=in0, in1=in1, op=op)


def TS(out, in0, s1, s2, op0, op1):
    return lambda e: e.tensor_scalar(out=out, in0=in0, scalar1=s1, scalar2=s2, op0=op0, op1=op1)


def STT(out, in0, scalar, in1, op0, op1):
    return lambda e: e.scalar_tensor_tensor(out=out, in0=in0, scalar=scalar, in1=in1, op0=op0, op1=op1)


def TSA(out, in0, s1):
    return lambda e: e.tensor_scalar_add(out, in0, s1)


def TSM(out, in0, s1):
    return lambda e: e.tensor_scalar_mul(out, in0, s1)


def CP(out, in_):
    return lambda e: e.tensor_copy(out=out, in_=in_)


def RCP(out, in_):
    return lambda e: e.reciprocal(out=out, in_=in_)


def DMA(out, in_):
    return lambda e: e.dma_start(out=out, in_=in_)


class _Stop(Exception):
    pass


def build_program(debug_stop=None):
    nc = bass.Bass("TRN2", target_bir_lowering=False)

    def chk(name):
        if debug_stop is not None and name == debug_stop:
            raise _Stop()

    def din(name, shape, dt=F32):
        return nc.dram_tensor(name, list(shape), dt, kind="ExternalInput").ap()

    def dout(name, shape, dt=F32):
        return nc.dram_tensor(name, list(shape), dt, kind="ExternalOutput").ap()

    d_x = din("xT", [128, 2, NKC, T])
    d_wada = din("w_ada", [NL, D_MODEL, 6 * D_MODEL])
    d_win = din("w_in", [NL, D_MODEL, IN_TOTAL])
    d_wbr = [din("w_branch_conv", [NL, 1024, D_MODEL]), din("w_branch_win", [NL, 1024, D_MODEL]),
             din("w_branch_diff", [NL, 1024, D_MODEL])]
    d_wout = din("w_out", [NL, D_MODEL, D_MODEL])
    d_w1 = din("w_mlp1", [NL, D_MODEL, 4 * D_MODEL])
    d_w2 = din("w_mlp2", [NL, 4 * D_MODEL, D_MODEL])
    d_par = din("par", [128, NPAR])
    d_rope = din("rope", [128, 4, T])
    d_cmask = din("cmask", [128, 2, T], BF16)
    d_perm = din("perm", [128, 2, 128], BF16)
    d_ckw = din("ckw", [NL, 128, 2, 512])
    d_cvw = din("cvw", [NL, 512, 256])
    d_ckd = din("ckd", [NL, 8, 128, 512])
    d_cvd = din("cvd", [NL, 512, 1024])
    o_y = dout("yT", [128, 2, NKC, T])
    o_kw = dout("kwT", [128, NL, 2, T])
    o_vw = dout("vw", [NL, T, 256])
    o_kd = dout("kdT", [128, NL, 8, T])
    o_vd = dout("vd", [NL, T, 1024])
    sendW = [nc.dram_tensor(f"sendW{l}", [NRW, 512], BF16) for l in range(NL)]
    gathW = [nc.dram_tensor(f"gathW{l}", [4 * NRW, 512], BF16) for l in range(NL)]
    sendK = [nc.dram_tensor(f"sendK{l}", [1024, 512], BF16) for l in range(NL)]
    gathK = [nc.dram_tensor(f"gathK{l}", [4096, 512], BF16) for l in range(NL)]
    sendV = [nc.dram_tensor(f"sendV{l}", [1024, 512], BF16) for l in range(NL)]
    gathV = [nc.dram_tensor(f"gathV{l}", [4096, 512], BF16) for l in range(NL)]
    sendP = [nc.dram_tensor(f"sendP{l}", [8, 512], BF16) for l in range(NL)]
    gathP = [nc.dram_tensor(f"gathP{l}", [32, 512], BF16) for l in range(NL)]

    P = Prog(nc)
    xT = P.sb("xT_s", [128, 2, NKC, T], F32)
    hT = P.sb("hT", [128, NKC, T], BF16)
    wbufs = [P.sb(f"wb{i}", [128, KSUB, 512], BF16) for i in range(NWB)]
    arena = P.sb("arena", [128, 24576], BF16)
    pb = [P.sb("pb0", [128, 2, 258], F32), P.sb("pb1", [128, 1, 514], F32)]
    tmp = {k: P.sb(k, [128, 512], F32) for k in ("tA", "tB", "tC", "tD")}
    rstd = P.sb("rstd", [128, 512], F32)
    Eb = [P.sb(f"E{i}", [128, 2, 512], BF16) for i in range(3)]
    xb = [P.sb(f"xb{i}", [128, 512], BF16) for i in range(2)]
    sqb = P.sb("sqb", [128, 512], BF16)
    kvk = [P.sb(f"kvk{i}", [128, 1024], BF16) for i in range(3)]
    kvv = [P.sb(f"kvv{i}", [128, 8, 128], BF16) for i in range(3)]
    par = P.sb("par_s", [128, NPAR], F32)
    rope = P.sb("rope_s", [128, 4, T], F32)
    cmask = P.sb("cmask_s", [128, 2, T], BF16)
    perm = P.sb("perm_s", [128, 2, 128], BF16)
    ones = P.sb("ones", [128, 128], BF16)
    epst = P.sb("epst", [128, 1], F32)
    svec = P.sb("svec", [128, NKC, 2], BF16)
    svf = P.sb("svf", [128, NKC * 2], F32)
    mod = P.sb("mod", [128, NL, 2, 96], F32)
    AB = P.sb("AB", [128, NL, 2, 2, NKC], F32)
    es = P.sb("es", [128, NL * 8], F32)
    lamt = P.sb("lamt", [128, 16], F32)
    lprod = P.sb("lprod", [128, 4, 64], F32)
    phs = P.sb("phs", [128, 16], BF16)
    pha = P.sb("pha", [128, 4, 16], BF16)
    cbb = P.sb("cbb", [128, 8, 2], F32)
    fix = P.sb("fix", [128, 4, 8], F32)
    ps = P.ps("ps", [128, 8, 512], F32)

    def av(a, b, dt=None, pat=None, **kw):
        v = arena[:, a:b]
        if dt is not None:
            v = v.bitcast(dt)
        if pat is not None:
            v = v.rearrange(pat, **kw)
        return v
    qw = av(0, 4096, None, "p (h t) -> p h t", t=T)
    kw_ = av(4096, 5120, None, "p (h t) -> p h t", t=T)
    vw_ = av(5120, 6144, None, "p (c f) -> p c f", f=256)
    kcw = av(6144, 7168, None, "p (h t) -> p h t", t=T)
    vcw = av(7168, 8192, None, "p (c f) -> p c f", f=256)
    khalo = av(8192, 10240, None, "p (h r k) -> p h r k", h=2, r=8)
    vhalo = av(10240, 12288, None, "p (r f) -> p r f", f=256)
    qd = av(0, 4096, None, "p (h t) -> p h t", t=T)
    kd = av(4096, 8192, None, "p (h t) -> p h t", t=T)
    vd_ = av(8192, 12288, None, "p (c f) -> p c f", f=1024)
    merged = av(0, 8192, None, "p (c t) -> p c t", t=T)
    y_a = av(12288, 16384, None, "p (c t) -> p c t", t=T)
    y_b = av(16384, 20480, None, "p (c t) -> p c t", t=T)
    y_c = av(20480, 24576, None, "p (c t) -> p c t", t=T)
    cbS = av(16384, 20480, F32, "p (c t) -> p c t", t=T)
    ccS = av(20480, 24576, F32, "p (c t) -> p c t", t=T)
    act = av(0, 16384, None, "p (c t) -> p c t", t=T)
    R0a, R0b, R0c, YA, YB, YC = 'R0a', 'R0b', 'R0c', 'YA', 'YB', 'YC'
    H = [('h', kc) for kc in range(NKC)]

    def psk(b):
        return f'ps{b}'

    def bank(b, n=512):
        return ps[:, b, 0:n]

    def pcol(c):
        return par[:, c:c + 1]

    def emit():
        P.dma('sp', [DMA(par[:], d_par)], writes=['par'])
        P.dma('sp', [DMA(rope[:], d_rope)], writes=['rope'])
        P.dma('sp', [DMA(cmask[:], d_cmask), DMA(perm[:], d_perm)], writes=['cmask'])
        P.dma('sp', [DMA(xT[:, g, kc * 8:(kc + 1) * 8, :], d_x[:, g, kc * 8:(kc + 1) * 8, :]) for g in range(2) for kc in range(2)],
              writes=[('x', g, kc) for g in range(2) for kc in range(NKC)], semkey='D_x')
        P.op('dve', lambda e: e.memset(ones[:], 1.0), writes=['ones'])
        P.op('dve', lambda e: e.memset(epst[:], EPS), writes=['eps'])
        P.op('dve', lambda e: e.memset(pb[0][:], 0.0), writes=['pb0'])
        P.op('dve', lambda e: e.memset(pb[1][:], 0.0), writes=['pb1'])
        P.op('act', ACT(svf[:], par[:, P_CV:P_CV + 32], AF.Silu), reads=['par'], writes=['svf'])
        P.op('dve', CP(svec[:].rearrange("p k v -> p (k v)"), svf[:]), reads=['svf'], writes=['svec'])
        P.op('act', ACT(es[:], par[:, P_SINK:P_SINK + 16], AF.Exp), reads=['par'], writes=['es'])
        P.op('dve', TT(lprod[:, 0:2, :], par[:, P_LQ1:P_LQ1 + 128].rearrange("p (l i) -> p l i", l=2),
                       par[:, P_LK1:P_LK1 + 128].rearrange("p (l i) -> p l i", l=2), ALU.mult), reads=['par'], writes=['lprodA'])
        P.op('dve', TT(lprod[:, 2:4, :], par[:, P_LQ2:P_LQ2 + 128].rearrange("p (l i) -> p l i", l=2),
                       par[:, P_LK2:P_LK2 + 128].rearrange("p (l i) -> p l i", l=2), ALU.mult), reads=['par'], writes=['lprodB'])
        P.op('dve', lambda e: e.reduce_sum(out=lamt[:, 0:4], in_=lprod[:], axis=mybir.AxisListType.X),
             reads=['lprodA', 'lprodB'], writes=['lamt'])
        P.op('act', ACT(lamt[:, 4:8], lamt[:, 0:4], AF.Exp), reads=['lamt'], writes=['lamt2'])
        for l in range(NL):
            P.op('dve', STT(lamt[:, 8 + l:9 + l], lamt[:, 6 + l:7 + l], -LAM_INIT[l], lamt[:, 4 + l:5 + l], ALU.add, ALU.subtract),
                 reads=['lamt2'], writes=[('nlam', l)])
            P.op('dve', TS(lamt[:, 10 + l:11 + l], par[:, P_SUB + l:P_SUB + l + 1], 1.0 - LAM_INIT[l], 0.0, ALU.mult, ALU.add),
                 reads=['par'], writes=[('gsub', l)])

        wstate = {'i': 0}

        def wload(src, ksub, ncol):
            i = wstate['i'] % NWB
            wstate['i'] += 1
            key = f'wb{i}'
            P.dma('pool', [DMA(wbufs[i][:, 0:ksub, 0:ncol], src.rearrange("(k p) c -> p k c", p=128))], writes=[key], semkey='D_' + key)
            return wbufs[i], key

        def fm_tile(W2d, col0, ncol, K, rhs_fn, rhs_reads, banks, evac, N=512):
            nkc = K // 128
            nf = ncol // 128
            for ks in range(nkc // KSUB):
                buf, key = wload(W2d[ks * KSUB * 128:(ks + 1) * KSUB * 128, col0:col0 + ncol], KSUB, ncol)
                fns = []
                for j in range(nf):
                    for kk in range(KSUB):
                        kc = ks * KSUB + kk
                        fns.append(MM(bank(banks[j], N), buf[:, kk, j * 128:(j + 1) * 128], rhs_fn(kc), kc == 0, kc == nkc - 1))
                P.mm(fns, reads=[key] + rhs_reads, writes=[psk(banks[j]) for j in range(nf)])
            for j in range(nf):
                evac(j, banks[j])

        chk('const')
        for l in range(NL):
            P.op('dve', lambda e, l=l: e.memset(ps[:, l, 0:192], 0.0), writes=[psk(l)])
            for ct in range(24):
                for ks in range(2):
                    buf, key = wload(d_wada[l, ks * 1024:(ks + 1) * 1024, ct * 512:(ct + 1) * 512], KSUB, 512)
                    fns = []
                    for j in range(4):
                        fc = ct * 4 + j
                        for kk in range(KSUB):
                            kc = ks * KSUB + kk
                            fns.append(lambda e, o=ps[:, l, fc * 2:fc * 2 + 2], w=buf[:, kk, j * 128:(j + 1) * 128], r=svec[:, kc, :], sp=(kc == NKC - 1):
                                       e.matmul(o, lhsT=w, rhs=r, start=False, stop=sp, skip_group_check=True))
                    P.mm(fns, reads=[key, 'svec'], writes=[psk(l)])
            for v in range(2):
                P.op('dve', TT(mod[:, l, v, :], ps[:, l, 0:192].rearrange("p (f v) -> p f v", v=2)[:, :, v],
                               par[:, P_BA + l * 96:P_BA + (l + 1) * 96], ALU.add), reads=[psk(l), 'par'], writes=[('mod', l, v)])
                P.op('dve', STT(AB[:, l, v, 0, :], mod[:, l, v, 16:32], 1.0, par[:, P_NMIX + l * 16:P_NMIX + (l + 1) * 16], ALU.add, ALU.mult),
                     reads=[('mod', l, v), 'par'], writes=[('AB', l, v)])
                P.op('dve', STT(AB[:, l, v, 1, :], mod[:, l, v, 64:80], 1.0, par[:, P_NMLP + l * 16:P_NMLP + (l + 1) * 16], ALU.add, ALU.mult),
                     reads=[('mod', l, v), 'par'], writes=[('AB', l, v)])

        tmp_rot = {'i': 0}

        def nexttmp():
            k = ("tA", "tB")[tmp_rot['i'] % 2]
            tmp_rot['i'] += 1
            return tmp[k], k

        def norm_stats(g):
            for kc in range(NKC):
                P.op('dve', TT(hT[:, kc, :], xT[:, g, kc, :], xT[:, g, kc, :], ALU.mult), reads=[('x', g, kc)], writes=[('h', kc)])
            P.mm([MM(bank(7), ones[:], hT[:, kc, :], kc == 0, kc == NKC - 1) for kc in range(NKC)], reads=H + ['ones'], writes=[psk(7)])
            P.op('act', ACT(rstd[:], bank(7), AF.Ln, scale=1.0 / D_MODEL, bias=epst[:, 0:1]), reads=[psk(7), 'eps'], writes=['rstd'])
            P.op('act', ACT(rstd[:], rstd[:], AF.Exp, scale=-0.5), reads=['rstd'], writes=['rstd'])

        def norm_to_h(l, g, which):
            norm_stats(g)
            boff = 0 if which == 0 else 48
            for kc in range(NKC):
                t, tk = nexttmp()
                P.op('dve', STT(t[:], xT[:, g, kc, :], AB[:, l, g, which, kc:kc + 1], rstd[:], ALU.mult, ALU.mult),
                     reads=[('x', g, kc), 'rstd', ('AB', l, g)], writes=[tk])
                P.op('act', ACT(hT[:, kc, :], t[:], AF.Identity, bias=mod[:, l, g, boff + kc:boff + kc + 1]),
                     reads=[tk, ('mod', l, g)], writes=[('h', kc)])

        rot = {'xs': 0, 'xb': 0, 'E': 0, 'S': 0, 'it': 0, 'kv': 0}

        def rope_evac(b, dest, dkey, ci, pi):
            i = rot['xb'] % 2
            rot['xb'] += 1
            xsb = 4 + rot['xs'] % 4
            rot['xs'] += 1
            P.op('act', ACT(xb[i][:], bank(b), AF.Copy), reads=[psk(b)], writes=[('xb', i)])
            P.mm([MM(bank(xsb), perm[:, pi, :], xb[i][:], True, True)], reads=[('xb', i), 'cmask'], writes=[psk(xsb)])
            P.op('dve', TT(tmp['tC'][:], bank(b), rope[:, ci, :], ALU.mult), reads=[psk(b), 'rope'], writes=['tC'])
            P.op('dve', TT(tmp['tD'][:], bank(xsb), rope[:, ci + 1, :], ALU.mult), reads=[psk(xsb), 'rope'], writes=['tD'])
            P.op('dve', TT(dest, tmp['tC'][:], tmp['tD'][:], ALU.add), reads=['tC', 'tD'], writes=[dkey])

        def stage_out(b, n, dst, src_is_bank=True):
            t, tk = nexttmp()
            P.op('dve', CP(t[:, 0:n], bank(b, n)), reads=[psk(b)], writes=[tk])
            P.dma('sp', [DMA(dst, t[:, 0:n])], reads=[tk], writes=['OUT'], out=True)

        hrhs = lambda kc: hT[:, kc, :]

        def run_pass(l, g):
            Win = d_win[l]
            norm_to_h(l, g, 0)
            for half in range(2):
                def ev_q(j, b, half=half):
                    h = half * 4 + j
                    if g == 0:
                        P.op('act', ACT(qw[:, h, :], bank(b), AF.Copy), reads=[psk(b)], writes=[R0a])
                    else:
                        rope_evac(b, qw[:, h, :], R0a, 0, 0)
                fm_tile(Win, WQ + half * 512, 512, D_MODEL, hrhs, H, [0, 1, 2, 3], ev_q)
            chk(f'wq{l}{g}')
            for ks in range(2):
                buf, key = wload(Win[ks * 1024:(ks + 1) * 1024, WK:WK + 512], KSUB, 512)
                fns = []
                for j in range(2):
                    for kk in range(KSUB):
                        kc = ks * KSUB + kk
                        fns.append(MM(bank(j), buf[:, kk, j * 128:(j + 1) * 128], hT[:, kc, :], kc == 0, kc == NKC - 1))
                for tt in range(4):
                    for kk in range(KSUB):
                        kc = ks * KSUB + kk
                        fns.append(MM(bank(2 + tt, 256), hT[:, kc, tt * 128:(tt + 1) * 128], buf[:, kk, 256:512], kc == 0, kc == NKC - 1))
                P.mm(fns, reads=[key] + H, writes=[psk(b) for b in range(6)])
            chk(f'wkvmm{l}{g}')
            for tt in range(4):
                P.op('act', ACT(vw_[:, tt, :], bank(2 + tt, 256), AF.Copy), reads=[psk(2 + tt)], writes=[R0b])
                if g == 0:
                    stage_out(2 + tt, 256, o_vw[l, tt * 128:(tt + 1) * 128, :])
            chk(f'kwev{l}{g}')
            for j in range(2):
                if g == 0:
                    P.op('act', ACT(kw_[:, j, :], bank(j), AF.Copy), reads=[psk(j)], writes=[R0b])
                    stage_out(j, 512, o_kw[:, l, j, :])
                else:
                    rope_evac(j, kw_[:, j, :], R0b, 0, 0)
            if g == 1:
                sw = sendW[l].ap()
                kview = lambda r0: sw[r0:r0 + 64, :].rearrange("r (a c) -> (r a) c", c=128)
                vview = lambda r0: sw[r0:r0 + 64, :].rearrange("r (a c) -> (r a) c", c=256)
                fns = []
                for hk in range(2):
                    fns.append(DMA(kview(0)[hk * 128:(hk + 1) * 128, :], kw_[:, hk, 0:128]))
                    fns.append(DMA(kview(64)[hk * 128:(hk + 1) * 128, :], kw_[:, hk, 384:512]))
                fns.append(DMA(vview(128), vw_[:, 0, :]))
                fns.append(DMA(vview(192), vw_[:, 3, :]))
                P.dma('sp', fns, reads=[R0b], writes=[('sendW', l)])
                if _DEBUG_NOCC:
                    P.dma('pool', [DMA(gathW[l].ap()[0:NRW, :], sendW[l].ap())], reads=[('sendW', l)], writes=[('gathW', l)], semkey=f'CCW{l}')
                else:
                    P.dma('pool', [lambda e: e.collective_compute("AllGather", ALU.bypass, replica_groups=(_GROUPS_OVERRIDE or GROUPS),
                                                                  ins=[sendW[l].ap()], outs=[gathW[l].ap()])],
                          reads=[('sendW', l)], writes=[('gathW', l)], semkey=f'CCW{l}', inc=1)
            chk(f'winproj{l}{g}')
            pv = pb[g]
            ns, nt = (2, 256) if g == 0 else (1, 512)
            for half in range(2):
                def ev_cb(j, b):
                    P.op('act', ACT(cbS[:, j, :], bank(b), AF.Copy), reads=[psk(b)], writes=[YB])

                def ev_cc(j, b):
                    P.op('act', ACT(ccS[:, j, :], bank(b), AF.Copy), reads=[psk(b)], writes=[YC])

                def ev_cx(j, b, half=half):
                    i = half * 4 + j
                    cw = lambda tap: pcol(P_CW + (l * 3 + tap) * 8 + i)
                    v3 = lambda ap: ap.rearrange("p (s t) -> p s t", s=ns)
                    P.op('dve', TT(pv[:, :, 1:1 + nt], v3(bank(b)), v3(ccS[:, j, :]), ALU.mult), reads=[psk(b), YC], writes=[('pb', g)])
                    tC, tD = tmp['tC'][:], tmp['tD'][:]
                    P.op('dve', TSM(v3(tC), pv[:, :, 1:1 + nt], cw(1)), reads=[('pb', g), 'par'], writes=['tC'])
                    P.op('dve', STT(v3(tD), pv[:, :, 0:nt], cw(0), v3(tC), ALU.mult, ALU.add), reads=[('pb', g), 'tC', 'par'], writes=['tD'])
                    P.op('dve', STT(v3(tC), pv[:, :, 2:2 + nt], cw(2), v3(tD), ALU.mult, ALU.add), reads=[('pb', g), 'tD', 'par'], writes=['tC'])
                    P.op('dve', TT(y_a[:, i, :], tC, cbS[:, j, :], ALU.mult), reads=['tC', YB], writes=[YA])
                    if g == 1:
                        P.op('dve', CP(phs[:, i:i + 1], pv[:, 0, 1:2]), reads=[('pb', g)], writes=['phs'])
                        P.op('dve', CP(phs[:, 8 + i:9 + i], pv[:, 0, 512:513]), reads=[('pb', g)], writes=['phs'])
                        P.op('dve', CP(cbb[:, i, 0:1], cbS[:, j, 0:1]), reads=[YB], writes=['cbb'])
                        P.op('dve', CP(cbb[:, i, 1:2], cbS[:, j, 511:512]), reads=[YB], writes=['cbb'])
                fm_tile(Win, CB + half * 512, 512, D_MODEL, hrhs, H, [0, 1, 2, 3], ev_cb)
                fm_tile(Win, CC + half * 512, 512, D_MODEL, hrhs, H, [4, 5, 6, 7], ev_cc)
                fm_tile(Win, CX + half * 512, 512, D_MODEL, hrhs, H, [0, 1, 2, 3], ev_cx)
            chk(f'conv{l}{g}')
            sc_w = 128 ** -0.5
            if g == 1:
                gw = gathW[l].ap().rearrange("(r n) c -> r n c", r=4)
                P.dma('pool', [DMA(kcw[:, :, :], d_ckw[l]),
                               DMA(vcw[:, :, :], d_cvw[l].rearrange("(c p) f -> p c f", p=128))], writes=[R0b], semkey='D_cw')
                fns = []
                for part, r0 in ((0, 64), (1, 0)):
                    kv4 = gw[:, r0:r0 + 64, :].rearrange("r n (a c) -> r (n a) c", c=128)
                    for hk in range(2):
                        fns.append(DMA(khalo[:, hk, part * 4:(part + 1) * 4, :], kv4[:, hk * 128:(hk + 1) * 128, :].rearrange("r d c -> d r c")))
                for part, r0 in ((0, 192), (1, 128)):
                    vv4 = gw[:, r0:r0 + 64, :].rearrange("r n (a c) -> r (n a) c", c=256)
                    fns.append(DMA(vhalo[:, part * 4:(part + 1) * 4, :], vv4.rearrange("r t c -> t r c")))
                P.dma('sp', fns, reads=[('gathW', l)], writes=[R0c])

            def win_block(hk, qsl, ysl, chunks):
                it = rot['it']
                rot['it'] += 1
                ob, db = 2 + it % 2, 4 + it % 2
                rhs_q = qw[:, 4 * hk:4 * hk + 4, qsl]
                n = len(chunks)
                for ci, (k_ap, v_ap, m_ap, s_ap, rd) in enumerate(chunks):
                    sb_ = rot['S'] % 2
                    rot['S'] += 1
                    ei = rot['E'] % 3
                    rot['E'] += 1
                    E = Eb[ei][:, 0, :]
                    P.mm([MM(bank(sb_), k_ap, rhs_q, True, True)], reads=[R0a] + rd, writes=[psk(sb_)])
                    P.op('act', ACT(E, bank(sb_), AF.Exp, scale=sc_w), reads=[psk(sb_)], writes=[('E', ei)])
                    if m_ap is not None:
                        if s_ap is None:
                            P.op('dve', TT(E, E, m_ap, ALU.mult), reads=[('E', ei), 'cmask'], writes=[('E', ei)])
                        else:
                            P.op('dve', STT(E, E, s_ap, m_ap, ALU.mult, ALU.mult), reads=[('E', ei), 'cmask', 'par'], writes=[('E', ei)])
                    P.mm([MM(bank(ob), v_ap, E, ci == 0, ci == n - 1), MM(bank(db), ones[:], E, ci == 0, ci == n - 1)],
                         reads=[('E', ei), 'ones'] + rd, writes=[psk(ob), psk(db)])
                tC = tmp['tC'][:]
                for gg in range(4):
                    h = 4 * hk + gg
                    P.op('dve', TSA(tC[:, gg * 128:(gg + 1) * 128], ps[:, db, gg * 128:(gg + 1) * 128], es[:, l * 8 + h:l * 8 + h + 1]),
                         reads=[psk(db), 'es'], writes=['tC'])
                P.op('dve', RCP(tC, tC), reads=['tC'], writes=['tC'])
                P.op('dve', TT(y_b[:, 4 * hk:4 * hk + 4, ysl], bank(ob).rearrange("p (g q) -> p g q", g=4),
                               tC.rearrange("p (g q) -> p g q", g=4), ALU.mult), reads=[psk(ob), 'tC'], writes=[YB])

            mL, mR = cmask[:, 0, :], cmask[:, 1, :]
            if g == 0:
                for s in range(2):
                    for hk in range(2):
                        for qb in range(2):
                            sl = slice(s * 256 + qb * 128, s * 256 + qb * 128 + 128)
                            chunks = [(kw_[:, hk, s * 256 + c * 128:s * 256 + (c + 1) * 128], vw_[:, s * 2 + c, hk * 128:(hk + 1) * 128], None, None, [R0b])
                                      for c in range(2)]
                            win_block(hk, sl, sl, chunks)
            else:
                for hk in range(2):
                    for qb in range(4):
                        sl = slice(qb * 128, (qb + 1) * 128)
                        chunks = [(kcw[:, hk, c * 128:(c + 1) * 128], vcw[:, c, hk * 128:(hk + 1) * 128], None, None, [R0b]) for c in range(4)]
                        for cc in (qb - 1, qb, qb + 1):
                            if 0 <= cc <= 3:
                                m = mL if cc == qb - 1 else (mR if cc == qb + 1 else None)
                                chunks.append((kw_[:, hk, cc * 128:(cc + 1) * 128], vw_[:, cc, hk * 128:(hk + 1) * 128], m, None, [R0b]))
                        if qb == 0:
                            for r in range(4):
                                chunks.append((khalo[:, hk, r, :], vhalo[:, r, hk * 128:(hk + 1) * 128], mL, pcol(P_SELL + r), [R0c]))
                        if qb == 3:
                            for r in range(4):
                                chunks.append((khalo[:, hk, 4 + r, :], vhalo[:, 4 + r, hk * 128:(hk + 1) * 128], mR, pcol(P_SELR + r), [R0c]))
                        win_block(hk, sl, sl, chunks)
            chk(f'winattn{l}{g}')
            for half in range(2):
                def ev_dq(j, b, half=half):
                    h = half * 4 + j
                    if g == 0:
                        P.op('act', ACT(qd[:, h, :], bank(b), AF.Copy), reads=[psk(b)], writes=[R0a])
                    else:
                        rope_evac(b, qd[:, h, :], R0a, 2, 1)
                fm_tile(Win, DQ + half * 512, 512, D_MODEL, hrhs, H, [0, 1, 2, 3], ev_dq)
            for half in range(2):
                def ev_dk(j, b, half=half):
                    h = half * 4 + j
                    if g == 0:
                        P.op('act', ACT(kd[:, h, :], bank(b), AF.Copy), reads=[psk(b)], writes=[R0b])
                        stage_out(b, 512, o_kd[:, l, h, :])
                    else:
                        rope_evac(b, kd[:, h, :], R0b, 2, 1)
                fm_tile(Win, DK + half * 512, 512, D_MODEL, hrhs, H, [0, 1, 2, 3], ev_dk)
            for half in range(2):
                banks = [0, 1, 2, 3] if half == 0 else [4, 5, 6, 7]
                for ks in range(2):
                    buf, key = wload(Win[ks * 1024:(ks + 1) * 1024, DV + half * 512:DV + (half + 1) * 512], KSUB, 512)
                    fns = []
                    for tt in range(4):
                        for kk in range(KSUB):
                            kc = ks * KSUB + kk
                            fns.append(MM(bank(banks[tt]), hT[:, kc, tt * 128:(tt + 1) * 128], buf[:, kk, :], kc == 0, kc == NKC - 1))
                    P.mm(fns, reads=[key] + H, writes=[psk(b) for b in banks])
                for tt in range(4):
                    P.op('act', ACT(vd_[:, tt, half * 512:(half + 1) * 512], bank(banks[tt]), AF.Copy), reads=[psk(banks[tt])], writes=[R0c])
                    if g == 0:
                        stage_out(banks[tt], 512, o_vd[l, tt * 128:(tt + 1) * 128, half * 512:(half + 1) * 512])
            if g == 1:
                for nm, snd, gth, src_fn, rd in (
                        ('K', sendK[l], gathK[l], lambda: DMA(sendK[l].ap().rearrange("(h d) t -> d h t", d=128), kd[:, :, :]), [R0b]),
                        ('V', sendV[l], gathV[l], lambda: DMA(sendV[l].ap().rearrange("(t two) c -> t (two c)", two=2).rearrange("(c p) f -> p c f", p=128), vd_[:, :, :]), [R0c]),
                        ('P', sendP[l], gathP[l], lambda: DMA(sendP[l].ap()[0:4, :].rearrange("r (a c) -> (r a) c", c=16), phs[:]), ['phs'])):
                    P.dma('sp', [src_fn()], reads=rd, writes=[('send' + nm, l)])
                    if _DEBUG_NOCC:
                        P.dma('pool', [DMA(gth.ap()[0:snd.ap().shape[0], :], snd.ap())], reads=[('send' + nm, l)], writes=[('gath' + nm, l)], semkey=f'CC{nm}{l}')
                    else:
                        P.dma('pool', [lambda e, snd=snd, gth=gth: e.collective_compute("AllGather", ALU.bypass, replica_groups=(_GROUPS_OVERRIDE or GROUPS),
                                                                                        ins=[snd.ap()], outs=[gth.ap()])],
                              reads=[('send' + nm, l)], writes=[('gath' + nm, l)], semkey=f'CC{nm}{l}', inc=1)
            chk(f'diffproj{l}{g}')
            sc_d = 64 ** -0.5
            nlam = lamt[:, 8 + l:9 + l]
            gsub = lamt[:, 10 + l:11 + l]

            def diff_finish(h, tsl, n, t0, t1):
                tD = tmp['tD'][:, 0:n]
                P.op('dve', STT(tD, t1, nlam, t0, ALU.mult, ALU.add), reads=['tA', 'tB', ('nlam', l)], writes=['tD'])
                P.op('dve', TT(sqb[:, 0:n], tD, tD, ALU.mult), reads=['tD'], writes=['sqb'])
                P.mm([MM(bank(0, n), ones[:], sqb[:, 0:n], True, True)], reads=['sqb', 'ones'], writes=[psk(0)])
                tC = tmp['tC'][:, 0:n]
                P.op('act', ACT(tC, bank(0, n), AF.Ln, scale=1.0 / 128, bias=epst[:, 0:1]), reads=[psk(0), 'eps'], writes=['tC'])
                P.op('act', ACT(tC, tC, AF.Exp, scale=-0.5), reads=['tC'], writes=['tC'])
                P.op('dve', STT(y_c[:, h, tsl], tD, gsub, tC, ALU.mult, ALU.mult), reads=['tD', 'tC', ('gsub', l)], writes=[YC])

            if g == 0:
                for s in range(2):
                    for h in range(8):
                        it = rot['it']
                        rot['it'] += 1
                        ob, db = 4 + it % 2, 6 + it % 2
                        tsl = slice(s * 256, (s + 1) * 256)
                        for c in range(2):
                            sb_ = 2 * (rot['S'] % 2)
                            rot['S'] += 1
                            ei = rot['E'] % 3
                            rot['E'] += 1
                            E = Eb[ei][:, :, 0:256]
                            ksl = slice(s * 256 + c * 128, s * 256 + (c + 1) * 128)
                            P.mm([MM(ps[:, sb_, 0:256], kd[0:64, h, ksl], qd[0:64, h, tsl], True, True),
                                  MM(ps[:, sb_ + 1, 0:256], kd[64:128, h, ksl], qd[64:128, h, tsl], True, True)],
                                 reads=[R0a, R0b], writes=[psk(sb_), psk(sb_ + 1)])
                            P.op('act', ACT(E, ps[:, sb_:sb_ + 2, 0:256], AF.Exp, scale=sc_d), reads=[psk(sb_), psk(sb_ + 1)], writes=[('E', ei)])
                            P.mm([MM(bank(ob), vd_[:, s * 2 + c, h * 128:(h + 1) * 128], E, c == 0, c == 1),
                                  MM(bank(db), ones[:], E, c == 0, c == 1)], reads=[('E', ei), 'ones', R0c], writes=[psk(ob), psk(db)])
                        tA, tB = tmp['tA'][:], tmp['tB'][:]
                        P.op('dve', RCP(tB, bank(db)), reads=[psk(db)], writes=['tB'])
                        P.op('dve', TT(tA, bank(ob), tB, ALU.mult), reads=[psk(ob), 'tB'], writes=['tA'])
                        diff_finish(h, tsl, 256, tA[:, 0:256], tA[:, 256:512])
            else:
                gd = gathK[l].ap().rearrange("(r n) c -> n r c", r=4)
                for h in range(8):
                    pieces = []
                    for pc in range(3):
                        si = rot['kv'] % 3
                        rot['kv'] += 1
                        kb, vb = kvk[si], kvv[si]
                        if pc == 0:
                            P.dma('pool', [DMA(kb[:, 0:512], d_ckd[l, h]),
                                           DMA(vb[:, 0:4, :], d_cvd[l].rearrange("(c p) f -> p c f", p=128)[:, :, h * 128:(h + 1) * 128])],
                                  writes=[('kv', si)], semkey=f'D_kv{si}')
                            nch = 4
                        else:
                            r0 = (pc - 1) * 2
                            fns = [DMA(kb[:, :].rearrange("p (r t) -> p r t", r=2), gd[h * 128:(h + 1) * 128, r0:r0 + 2, :])]
                            for rr in range(2):
                                vsrc = gathV[l].ap()[(r0 + rr) * 1024:(r0 + rr + 1) * 1024, :] \
                                    .rearrange("(t two) c -> t (two c)", two=2).rearrange("(c p) f -> p c f", p=128)
                                fns.append(DMA(vb[:, rr * 4:(rr + 1) * 4, :], vsrc[:, :, h * 128:(h + 1) * 128]))
                            P.dma('sp', fns, reads=[('gathK', l), ('gathV', l)], writes=[('kv', si)], semkey=f'D_kv{si}')
                            nch = 8
                        pieces.append((si, nch))
                    tot = 20
                    ci = 0
                    for si, nch in pieces:
                        kb, vb = kvk[si], kvv[si]
                        for c in range(nch):
                            sb_ = 2 * (rot['S'] % 2)
                            rot['S'] += 1
                            ei = rot['E'] % 3
                            rot['E'] += 1
                            P.mm([MM(bank(sb_), kb[0:64, c * 128:(c + 1) * 128], qd[0:64, h, :], True, True),
                                  MM(bank(sb_ + 1), kb[64:128, c * 128:(c + 1) * 128], qd[64:128, h, :], True, True)],
                                 reads=[R0a, ('kv', si)], writes=[psk(sb_), psk(sb_ + 1)])
                            P.op('act', ACT(Eb[ei][:], ps[:, sb_:sb_ + 2, :], AF.Exp, scale=sc_d), reads=[psk(sb_), psk(sb_ + 1)], writes=[('E', ei)])
                            st, sp_ = ci == 0, ci == tot - 1
                            P.mm([MM(bank(4), vb[:, c, :], Eb[ei][:, 0, :], st, sp_), MM(bank(5), vb[:, c, :], Eb[ei][:, 1, :], st, sp_),
                                  MM(bank(6), ones[:], Eb[ei][:, 0, :], st, sp_), MM(bank(7), ones[:], Eb[ei][:, 1, :], st, sp_)],
                                 reads=[('E', ei), 'ones', ('kv', si)], writes=[psk(4), psk(5), psk(6), psk(7)])
                            ci += 1
                    tA, tB, tC = tmp['tA'][:], tmp['tB'][:], tmp['tC'][:]
                    P.op('dve', RCP(tC, bank(6)), reads=[psk(6)], writes=['tC'])
                    P.op('dve', TT(tA, bank(4), tC, ALU.mult), reads=[psk(4), 'tC'], writes=['tA'])
                    P.op('dve', RCP(tC, bank(7)), reads=[psk(7)], writes=['tC'])
                    P.op('dve', TT(tB, bank(5), tC, ALU.mult), reads=[psk(5), 'tC'], writes=['tB'])
                    diff_finish(h, slice(0, 512), 512, tA, tB)
                P.dma('sp', [DMA(pha[:, r, :], gathP[l].ap()[r * 8:r * 8 + 4, :].rearrange("r (a c) -> (r a) c", c=16)) for r in range(4)],
                      reads=[('gathP', l)], writes=['pha'])
                for side, selc, src0 in ((0, P_SELL, 8), (1, P_SELR, 0)):
                    acc = fix[:, side, :]
                    P.op('dve', TSM(acc, pha[:, 0, src0:src0 + 8], pcol(selc)), reads=['pha', 'par'], writes=[('fix', side)])
                    for r in range(1, 4):
                        P.op('dve', STT(acc, pha[:, r, src0:src0 + 8], pcol(selc + r), acc, ALU.mult, ALU.add), reads=['pha', 'par', ('fix', side)], writes=[('fix', side)])
                    tap = 0 if side == 0 else 2
                    cwv = par[:, P_CW + (l * 3 + tap) * 8:P_CW + (l * 3 + tap) * 8 + 8]
                    d2 = fix[:, 2 + side, :]
                    P.op('dve', TT(d2, acc, cwv, ALU.mult), reads=[('fix', side), 'par'], writes=[('fix', 2 + side)])
                    P.op('dve', TT(d2, d2, cbb[:, :, side], ALU.mult), reads=[('fix', 2 + side), 'cbb'], writes=[('fix', 2 + side)])
                    col = 0 if side == 0 else 511
                    P.op('dve', TT(y_a[:, :, col], y_a[:, :, col], d2, ALU.add), reads=[YA, ('fix', 2 + side)], writes=[YA])
            chk(f'diffattn{l}{g}')
            for br, (ysrc, ykey, gcol) in enumerate(((y_a, YA, GA), (y_b, YB, GB), (y_c, YC, GC))):
                for ct in range(4):
                    def ev_gate(j, b):
                        P.op('act', ACT(Eb[j // 2][:, j % 2, :], bank(b), AF.Sigmoid), reads=[psk(b)], writes=[('E', j // 2)])
                    fm_tile(Win, gcol + ct * 512, 512, D_MODEL, hrhs, H, [0, 1, 2, 3], ev_gate)

                    def ev_br(j, b, ct=ct, br=br):
                        fc = ct * 4 + j
                        gt = Eb[j // 2][:, j % 2, :]
                        if br == 0:
                            P.op('dve', TT(merged[:, fc, :], bank(b), gt, ALU.mult), reads=[psk(b), ('E', j // 2)], writes=[R0a, R0b])
                        else:
                            tC = tmp['tC'][:]
                            P.op('dve', TT(tC, bank(b), gt, ALU.mult), reads=[psk(b), ('E', j // 2)], writes=['tC'])
                            P.op('dve', TT(merged[:, fc, :], merged[:, fc, :], tC, ALU.add), reads=['tC', R0a, R0b], writes=[R0a, R0b])
                    fm_tile(d_wbr[br][l], ct * 512, 512, 1024, lambda kc, ysrc=ysrc: ysrc[:, kc, :], [ykey], [4, 5, 6, 7], ev_br)
            chk(f'branch{l}{g}')
            for ct in range(4):
                def ev_o(j, b, ct=ct):
                    fc = ct * 4 + j
                    P.op('dve', STT(xT[:, g, fc, :], bank(b), mod[:, l, g, 32 + fc:33 + fc], xT[:, g, fc, :], ALU.mult, ALU.add),
                         reads=[psk(b), ('mod', l, g), ('x', g, fc)], writes=[('x', g, fc)])
                fm_tile(d_wout[l], ct * 512, 512, D_MODEL, lambda kc: merged[:, kc, :], [R0a, R0b], [0, 1, 2, 3] if ct % 2 == 0 else [4, 5, 6, 7], ev_o)
            chk(f'wout{l}{g}')
            norm_to_h(l, g, 1)
            tile_i = 0
            for half in range(2):
                for ct in range(8):
                    def ev_m1(j, b, ct=ct):
                        t, tk = nexttmp()
                        P.op('act', ACT(t[:], bank(b), AF.Relu), reads=[psk(b)], writes=[tk])
                        P.op('dve', TT(act[:, ct * 4 + j, :], t[:], t[:], ALU.mult), reads=[tk], writes=[R0a, R0b, R0c, YA])
                    fm_tile(d_w1[l], half * 4096 + ct * 512, 512, D_MODEL, hrhs, H, [0, 1, 2, 3] if tile_i % 2 == 0 else [4, 5, 6, 7], ev_m1)
                    tile_i += 1
                for ct in range(4):
                    def ev_m2(j, b, ct=ct):
                        fc = ct * 4 + j
                        P.op('dve', STT(xT[:, g, fc, :], bank(b), mod[:, l, g, 80 + fc:81 + fc], xT[:, g, fc, :], ALU.mult, ALU.add),
                             reads=[psk(b), ('mod', l, g), ('x', g, fc)], writes=[('x', g, fc)])
                    fm_tile(d_w2[l][half * 4096:(half + 1) * 4096, :], ct * 512, 512, 4096, lambda kc: act[:, kc, :], [R0a, R0b, R0c, YA],
                            [0, 1, 2, 3] if tile_i % 2 == 0 else [4, 5, 6, 7], ev_m2)
                    tile_i += 1

        def final_norm(g):
            norm_stats(g)
            for kc in range(NKC):
                t, tk = nexttmp()
                P.op('dve', STT(t[:], xT[:, g, kc, :], par[:, P_NFIN + kc:P_NFIN + kc + 1], rstd[:], ALU.mult, ALU.mult),
                     reads=[('x', g, kc), 'rstd', 'par'], writes=[tk])
                P.dma('sp', [DMA(o_y[:, g, kc, :], t[:])], reads=[tk], writes=['OUT'], out=True)

        chk('p0')
        for l in range(NL):
            for g in range(2):
                run_pass(l, g)
                chk(f'pass{l}{g}')
                if l == NL - 1:
                    final_norm(g)

    try:
        emit()
    except _Stop:
        pass
    P.finish()
    return nc


_NC_CACHE = {}
_DEBUG_STOP = None
_DEBUG_CORES = 8


def _rope_tables(j):
    t = np.arange(512 * j, 512 * j + 512)
    row = (t // 64).astype(np.float32)
    col = (t % 64).astype(np.float32)
    out = np.zeros((128, 4, 512), np.float32)
    inv32 = (10000.0 ** (-np.arange(32, dtype=np.float32) / 32)).astype(np.float32)
    inv16 = (10000.0 ** (-np.arange(16, dtype=np.float32) / 16)).astype(np.float32)
    for d in range(128):
        blk = d // 32
        pos = row if blk < 2 else col
        ang = (pos * inv32[d % 32]).astype(np.float32)
        out[d, 0] = np.cos(ang)
        out[d, 1] = -np.sin(ang) if blk % 2 == 0 else np.sin(ang)
        e = d % 64
        blk = e // 16
        pos = row if blk < 2 else col
        ang = (pos * inv16[e % 16]).astype(np.float32)
        out[d, 2] = np.cos(ang)
        out[d, 3] = -np.sin(ang) if blk % 2 == 0 else np.sin(ang)
    return out


def _const_tables():
    perm = np.zeros((128, 2, 128), np.float32)
    for d in range(128):
        pw = d + 32 if (d % 64) < 32 else d - 32
        pd = d + 16 if (d % 32) < 16 else d - 16
        perm[pw, 0, d] = 1.0
        perm[pd, 1, d] = 1.0
    i = np.arange(128)[:, None]
    q = np.arange(128)[None, :]
    mL = (i >= q).astype(np.float32)
    mR = (i <= q).astype(np.float32)
    cmask = np.stack([np.tile(mL, (1, 4)), np.tile(mR, (1, 4))], axis=1)
    return perm.astype(ml_dtypes.bfloat16), cmask.astype(ml_dtypes.bfloat16)


def kernel(x_prompt, x_sample, cache_win_k, cache_win_v, cache_diff_k, cache_diff_v, c, c_ctx,
           w_ada, b_ada, norm_mix, norm_mlp, w_in, conv_w, win_sink,
           lambda_q1, lambda_k1, lambda_q2, lambda_k2, diff_subln,
           w_branch_conv, w_branch_win, w_branch_diff, w_out, w_mlp1, w_mlp2, norm_final):
    f32 = lambda a: np.ascontiguousarray(np.asarray(a, dtype=np.float32))
    x_prompt, x_sample = f32(x_prompt), f32(x_sample)
    if 'nc' not in _NC_CACHE:
        _NC_CACHE['nc'] = build_program(_DEBUG_STOP)
    nc = _NC_CACHE['nc']
    perm, cmask = _const_tables()
    shared = {"w_ada": f32(w_ada), "w_in": f32(w_in), "w_branch_conv": f32(w_branch_conv), "w_branch_win": f32(w_branch_win),
              "w_branch_diff": f32(w_branch_diff), "w_out": f32(w_out), "w_mlp1": f32(w_mlp1), "w_mlp2": f32(w_mlp2),
              "perm": perm, "cmask": cmask}
    fm = lambda v: f32(v).reshape(-1, 128).T
    in_maps = []
    for core in range(8):
        s, j = core // 4, core % 4
        xp = x_prompt[2 * core:2 * core + 2].reshape(512, D_MODEL)
        xs = x_sample[s, 512 * j:512 * (j + 1)]
        xT = np.stack([xp.T.reshape(NKC, 128, 512), xs.T.reshape(NKC, 128, 512)], axis=0).transpose(2, 0, 1, 3)
        par = np.zeros((128, NPAR), np.float32)
        par[:, P_CV:P_CV + 32] = np.stack([fm(c_ctx), fm(f32(c)[s])], axis=2).reshape(128, 32)
        for l in range(NL):
            par[:, P_BA + l * 96:P_BA + (l + 1) * 96] = fm(f32(b_ada)[l])
            par[:, P_NMIX + l * 16:P_NMIX + (l + 1) * 16] = fm(f32(norm_mix)[l])
            par[:, P_NMLP + l * 16:P_NMLP + (l + 1) * 16] = fm(f32(norm_mlp)[l])
            for tap in range(3):
                par[:, P_CW + (l * 3 + tap) * 8:P_CW + (l * 3 + tap) * 8 + 8] = fm(f32(conv_w)[l, tap])
            par[:, P_SINK + l * 8:P_SINK + (l + 1) * 8] = f32(win_sink)[l][None, :]
            for off, arr in ((P_LQ1, lambda_q1), (P_LK1, lambda_k1), (P_LQ2, lambda_q2), (P_LK2, lambda_k2)):
                par[:, off + l * 64:off + (l + 1) * 64] = f32(arr)[l][None, :]
            par[:, P_SUB + l] = f32(diff_subln)[l]
        par[:, P_NFIN:P_NFIN + 16] = fm(norm_final)
        if j > 0:
            par[:, P_SELL + j - 1] = 1.0
        if j < 3:
            par[:, P_SELR + j + 1] = 1.0
        m = dict(shared)
        m["xT"] = np.ascontiguousarray(xT)
        m["par"] = par
        m["rope"] = _rope_tables(j)
        m["ckw"] = np.ascontiguousarray(f32(cache_win_k)[s].transpose(0, 3, 2, 1))
        m["cvw"] = np.ascontiguousarray(f32(cache_win_v)[s].reshape(NL, 512, 256))
        m["ckd"] = np.ascontiguousarray(f32(cache_diff_k)[s].transpose(0, 2, 3, 1))
        m["cvd"] = np.ascontiguousarray(f32(cache_diff_v)[s].reshape(NL, 512, 1024))
        in_maps.append(m)
    ncore = _DEBUG_CORES
    res = run_bass_kernel_spmd(nc, in_maps[:ncore], core_ids=list(range(ncore)))
    R = list(res.results) + [res.results[0]] * (8 - ncore)
    y_prompt = np.zeros((16, 256, D_MODEL), np.float32)
    y_sample = np.zeros((2, 2048, D_MODEL), np.float32)
    nwk = np.zeros((16, NL, 256, 2, 128), np.float32)
    nwv = np.zeros((16, NL, 256, 2, 128), np.float32)
    ndk = np.zeros((16, NL, 256, 8, 128), np.float32)
    ndv = np.zeros((16, NL, 256, 8, 128), np.float32)
    for core in range(8):
        s, j = core // 4, core % 4
        yT = np.asarray(R[core]["yT"], dtype=np.float32)
        yg = yT.transpose(1, 3, 2, 0).reshape(2, 512, D_MODEL)
        y_prompt[2 * core:2 * core + 2] = yg[0].reshape(2, 256, D_MODEL)
        y_sample[s, 512 * j:512 * (j + 1)] = yg[1]
        kwT = np.asarray(R[core]["kwT"], dtype=np.float32)
        nwk[2 * core:2 * core + 2] = kwT.transpose(3, 1, 2, 0).reshape(2, 256, NL, 2, 128).transpose(0, 2, 1, 3, 4)
        vw = np.asarray(R[core]["vw"], dtype=np.float32)
        nwv[2 * core:2 * core + 2] = vw.reshape(NL, 2, 256, 2, 128).transpose(1, 0, 2, 3, 4)
        kdT = np.asarray(R[core]["kdT"], dtype=np.float32)
        ndk[2 * core:2 * core + 2] = kdT.transpose(3, 1, 2, 0).reshape(2, 256, NL, 8, 128).transpose(0, 2, 1, 3, 4)
        vd = np.asarray(R[core]["vd"], dtype=np.float32)
        ndv[2 * core:2 * core + 2] = vd.reshape(NL, 2, 256, 8, 128).transpose(1, 0, 2, 3, 4)
    return (y_prompt, y_sample, nwk, nwv, ndk, ndv)
```

```python
from contextlib import ExitStack
import math
import numpy as np
import ml_dtypes
import concourse.bass as bass
import concourse.mybir as mybir
from concourse.bass_utils import run_bass_kernel_spmd

F32 = mybir.dt.float32
BF16 = mybir.dt.bfloat16
ALU = mybir.AluOpType
AF = mybir.ActivationFunctionType


class Prog:
    ENG = ('pe', 'act', 'dve', 'pool', 'sp')

    def __init__(self, nc):
        self.nc = nc
        self.stack = ExitStack()
        self.ops = {e: [] for e in self.ENG}
        self.sem = {}
        self.cnt = {}
        self.waited = {}
        self.last_w = {}
        self.readers = {}
        self.out_tokens = {}
        for e in self.ENG:
            self._mksem('E_' + e)

    def _mksem(self, name):
        if name not in self.sem:
            self.sem[name] = self.stack.enter_context(self.nc.semaphore(name))
            self.cnt[name] = 0
        return name

    def sb(self, name, shape, dt):
        return self.stack.enter_context(self.nc.sbuf_tensor(name, list(shape), dt))

    def ps(self, name, shape, dt=F32):
        return self.stack.enter_context(self.nc.psum_tensor(name, list(shape), dt))

    def _collect(self, eng, reads, writes):
        deps = {}

        def add(tok):
            if tok is None:
                return
            s, v, te = tok
            if te == 'pe' and eng == 'pe':
                return
            if deps.get(s, 0) < v:
                deps[s] = v
        for r in reads:
            add(self.last_w.get(r))
        for w in writes:
            add(self.last_w.get(w))
            for s, (v, te) in self.readers.get(w, {}).items():
                if te == eng and te != 'dma':
                    continue
                add((s, v, te))
        waits = []
        for s, v in deps.items():
            if self.waited.get((eng, s), 0) < v:
                self.waited[(eng, s)] = v
                waits.append((s, v))
        return waits

    def _record(self, tok, reads, writes):
        s, v, te = tok
        for r in reads:
            d = self.readers.setdefault(r, {})
            if d.get(s, (0, te))[0] <= v:
                d[s] = (v, te)
        for w in writes:
            self.last_w[w] = tok
            self.readers[w] = {}

    @staticmethod
    def _excl(reads, writes):
        pr = [r for r in reads if isinstance(r, str) and r.startswith('ps')]
        if not pr:
            return list(reads), list(writes)
        return [r for r in reads if r not in pr], list(writes) + [r for r in pr if r not in writes]

    def op(self, eng, fn, reads=(), writes=()):
        reads, writes = self._excl(reads, writes)
        waits = self._collect(eng, reads, writes)
        s = 'E_' + eng
        self.cnt[s] += 1
        self.ops[eng].append((waits, fn, (s, 1)))
        self._record((s, self.cnt[s], eng), reads, writes)

    def mm(self, fns, reads=(), writes=()):
        reads, writes = self._excl(reads, writes)
        waits = self._collect('pe', reads, writes)
        s = 'E_pe'
        self.cnt[s] += 1
        n = len(fns)
        for i, fn in enumerate(fns):
            self.ops['pe'].append((waits if i == 0 else [], fn, (s, 1) if i == n - 1 else None))
        self._record((s, self.cnt[s], 'pe'), reads, writes)

    def dma(self, q, fns, reads=(), writes=(), semkey=None, out=False, inc=16):
        if semkey is None:
            semkey = 'OUT' if out else 'D_' + str(writes[0])
        self._mksem(semkey)
        waits = self._collect(q, reads, writes)
        for i, fn in enumerate(fns):
            self.ops[q].append((waits if i == 0 else [], fn, (semkey, inc)))
        self.cnt[semkey] += inc * len(fns)
        tok = (semkey, self.cnt[semkey], 'dma')
        self._record(tok, reads, [] if out else writes)
        if out:
            self.out_tokens[q] = tok

    def finish(self):
        if 'OUT' in self.sem:
            v = self.cnt['OUT']
            self.ops['sp'].append(([('OUT', v)], None, None))
        nc = self.nc
        engs = {'pe': 'tensor', 'act': 'scalar', 'dve': 'vector', 'pool': 'gpsimd', 'sp': 'sync'}
        sem = self.sem
        with nc.Block() as block:
            for ename, bname in engs.items():
                oplist = self.ops[ename]

                def body(e, oplist=oplist):
                    for waits, fn, inc in oplist:
                        for s, v in waits:
                            e.wait_ge(sem[s], v)
                        if fn is None:
                            continue
                        ins = fn(e)
                        if inc is not None:
                            ins.then_inc(sem[inc[0]], inc[1])
                getattr(block, bname)(body)
        self.stack.close()


D_MODEL = 2048
NKC = 16
T = 512
NL = 2
KSUB = 8
NWB = 3
EPS = 1e-6
CB, CC, CX, WQ, WK, WV, DQ, DK, DV, GA, GB, GC = 0, 1024, 2048, 3072, 4096, 4352, 4608, 5632, 6656, 7680, 9728, 11776
IN_TOTAL = 13824
P_CV, P_BA, P_NMIX, P_NMLP, P_NFIN, P_CW, P_SINK, P_LQ1, P_LK1, P_LQ2, P_LK2, P_SUB, P_SELL, P_SELR, NPAR = \
    0, 32, 224, 256, 288, 304, 352, 368, 496, 624, 752, 880, 882, 886, 890
NRW = 256
NRD = 2052
LAM_INIT = [0.8 - 0.6 * math.exp(-0.3 * l) for l in range(NL)]
GROUPS = [[0, 1, 2, 3], [4, 5, 6, 7]]
_GROUPS_OVERRIDE = []
_DEBUG_NOCC = []


def MM(out, lhsT, rhs, start, stop):
    return lambda e: e.matmul(out, lhsT=lhsT, rhs=rhs, start=start, stop=stop)


def ACT(out, in_, func, **kw):
    return lambda e: e.activation(out=out, in_=in_, func=func, **kw)


def TT(out, in0, in1, op):
    return lambda e: e.tensor_tensor(out=out, in0=in0, in1=in1, op=op)


def TS(out, in0, s1, s2, op0, op1):
    return lambda e: e.tensor_scalar(out=out, in0=in0, scalar1=s1, scalar2=s2, op0=op0, op1=op1)


def STT(out, in0, scalar, in1, op0, op1):
    return lambda e: e.scalar_tensor_tensor(out=out, in0=in0, scalar=scalar, in1=in1, op0=op0, op1=op1)


def TSA(out, in0, s1):
    return lambda e: e.tensor_scalar_add(out, in0, s1)


def TSM(out, in0, s1):
    return lambda e: e.tensor_scalar_mul(out, in0, s1)


def CP(out, in_):
    return lambda e: e.tensor_copy(out=out, in_=in_)


def RCP(out, in_):
    return lambda e: e.reciprocal(out=out, in_=in_)


def DMA(out, in_):
    return lambda e: e.dma_start(out=out, in_=in_)


class _Stop(Exception):
    pass


def build_program(debug_stop=None):
    nc = bass.Bass("TRN2", target_bir_lowering=False)

    def chk(name):
        if debug_stop is not None and name == debug_stop:
            raise _Stop()

    def din(name, shape, dt=F32):
        return nc.dram_tensor(name, list(shape), dt, kind="ExternalInput").ap()

    def dout(name, shape, dt=F32):
        return nc.dram_tensor(name, list(shape), dt, kind="ExternalOutput").ap()

    d_x = din("xT", [128, 2, NKC, T])
    d_wada = din("w_ada", [NL, D_MODEL, 6 * D_MODEL])
    d_win = din("w_in", [NL, D_MODEL, IN_TOTAL])
    d_wbr = [din("w_branch_conv", [NL, 1024, D_MODEL]), din("w_branch_win", [NL, 1024, D_MODEL]),
             din("w_branch_diff", [NL, 1024, D_MODEL])]
    d_wout = din("w_out", [NL, D_MODEL, D_MODEL])
    d_w1 = din("w_mlp1", [NL, D_MODEL, 4 * D_MODEL])
    d_w2 = din("w_mlp2", [NL, 4 * D_MODEL, D_MODEL])
    d_par = din("par", [128, NPAR])
    d_rope = din("rope", [128, 4, T])
    d_cmask = din("cmask", [128, 2, T], BF16)
    d_perm = din("perm", [128, 2, 128], BF16)
    d_ckw = din("ckw", [NL, 128, 2, 512])
    d_cvw = din("cvw", [NL, 512, 256])
    d_ckd = din("ckd", [NL, 8, 128, 512])
    d_cvd = din("cvd", [NL, 512, 1024])
    o_y = dout("yT", [128, 2, NKC, T])
    o_kw = dout("kwT", [128, NL, 2, T])
    o_vw = dout("vw", [NL, T, 256])
    o_kd = dout("kdT", [128, NL, 8, T])
    o_vd = dout("vd", [NL, T, 1024])
    sendW = [nc.dram_tensor(f"sendW{l}", [NRW, 512], BF16) for l in range(NL)]
    gathW = [nc.dram_tensor(f"gathW{l}", [4 * NRW, 512], BF16) for l in range(NL)]
    sendK = [nc.dram_tensor(f"sendK{l}", [1024, 512], BF16) for l in range(NL)]
    gathK = [nc.dram_tensor(f"gathK{l}", [4096, 512], BF16) for l in range(NL)]
    sendV = [nc.dram_tensor(f"sendV{l}", [1024, 512], BF16) for l in range(NL)]
    gathV = [nc.dram_tensor(f"gathV{l}", [4096, 512], BF16) for l in range(NL)]
    sendP = [nc.dram_tensor(f"sendP{l}", [8, 512], BF16) for l in range(NL)]
    gathP = [nc.dram_tensor(f"gathP{l}", [32, 512], BF16) for l in range(NL)]

    P = Prog(nc)
    xT = P.sb("xT_s", [128, 2, NKC, T], F32)
    hT = P.sb("hT", [128, NKC, T], BF16)
    wbufs = [P.sb(f"wb{i}", [128, KSUB, 512], BF16) for i in range(NWB)]
    arena = P.sb("arena", [128, 24576], BF16)
    pb = [P.sb("pb0", [128, 2, 258], F32), P.sb("pb1", [128, 1, 514], F32)]
    tmp = {k: P.sb(k, [128, 512], F32) for k in ("tA", "tB", "tC", "tD")}
    rstd = P.sb("rstd", [128, 512], F32)
    Eb = [P.sb(f"E{i}", [128, 2, 512], BF16) for i in range(3)]
    xb = [P.sb(f"xb{i}", [128, 512], BF16) for i in range(2)]
    sqb = P.sb("sqb", [128, 512], BF16)
    kvk = [P.sb(f"kvk{i}", [128, 1024], BF16) for i in range(3)]
    kvv = [P.sb(f"kvv{i}", [128, 8, 128], BF16) for i in range(3)]
    par = P.sb("par_s", [128, NPAR], F32)
    rope = P.sb("rope_s", [128, 4, T], F32)
    cmask = P.sb("cmask_s", [128, 2, T], BF16)
    perm = P.sb("perm_s", [128, 2, 128], BF16)
    ones = P.sb("ones", [128, 128], BF16)
    epst = P.sb("epst", [128, 1], F32)
    svec = P.sb("svec", [128, NKC, 2], BF16)
    svf = P.sb("svf", [128, NKC * 2], F32)
    mod = P.sb("mod", [128, NL, 2, 96], F32)
    AB = P.sb("AB", [128, NL, 2, 2, NKC], F32)
    es = P.sb("es", [128, NL * 8], F32)
    lamt = P.sb("lamt", [128, 16], F32)
    lprod = P.sb("lprod", [128, 4, 64], F32)
    phs = P.sb("phs", [128, 16], BF16)
    pha = P.sb("pha", [128, 4, 16], BF16)
    cbb = P.sb("cbb", [128, 8, 2], F32)
    fix = P.sb("fix", [128, 4, 8], F32)
    ps = P.ps("ps", [128, 8, 512], F32)

    def av(a, b, dt=None, pat=None, **kw):
        v = arena[:, a:b]
        if dt is not None:
            v = v.bitcast(dt)
        if pat is not None:
            v = v.rearrange(pat, **kw)
        return v
    qw = av(0, 4096, None, "p (h t) -> p h t", t=T)
    kw_ = av(4096, 5120, None, "p (h t) -> p h t", t=T)
    vw_ = av(5120, 6144, None, "p (c f) -> p c f", f=256)
    kcw = av(6144, 7168, None, "p (h t) -> p h t", t=T)
    vcw = av(7168, 8192, None, "p (c f) -> p c f", f=256)
    khalo = av(8192, 10240, None, "p (h r k) -> p h r k", h=2, r=8)
    vhalo = av(10240, 12288, None, "p (r f) -> p r f", f=256)
    qd = av(0, 4096, None, "p (h t) -> p h t", t=T)
    kd = av(4096, 8192, None, "p (h t) -> p h t", t=T)
    vd_ = av(8192, 12288, None, "p (c f) -> p c f", f=1024)
    merged = av(0, 8192, None, "p (c t) -> p c t", t=T)
    y_a = av(12288, 16384, None, "p (c t) -> p c t", t=T)
    y_b = av(16384, 20480, None, "p (c t) -> p c t", t=T)
    y_c = av(20480, 24576, None, "p (c t) -> p c t", t=T)
    cbS = av(16384, 20480, F32, "p (c t) -> p c t", t=T)
    ccS = av(20480, 24576, F32, "p (c t) -> p c t", t=T)
    act = av(0, 16384, None, "p (c t) -> p c t", t=T)
    R0a, R0b, R0c, YA, YB, YC = 'R0a', 'R0b', 'R0c', 'YA', 'YB', 'YC'
    H = [('h', kc) for kc in range(NKC)]

    def psk(b):
        return f'ps{b}'

    def bank(b, n=512):
        return ps[:, b, 0:n]

    def pcol(c):
        return par[:, c:c + 1]

    def emit():
        P.dma('sp', [DMA(par[:], d_par)], writes=['par'])
        P.dma('sp', [DMA(rope[:], d_rope)], writes=['rope'])
        P.dma('sp', [DMA(cmask[:], d_cmask), DMA(perm[:], d_perm)], writes=['cmask'])
        P.dma('sp', [DMA(xT[:, g, kc * 8:(kc + 1) * 8, :], d_x[:, g, kc * 8:(kc + 1) * 8, :]) for g in range(2) for kc in range(2)],
              writes=[('x', g, kc) for g in range(2) for kc in range(NKC)], semkey='D_x')
        P.op('dve', lambda e: e.memset(ones[:], 1.0), writes=['ones'])
        P.op('dve', lambda e: e.memset(epst[:], EPS), writes=['eps'])
        P.op('dve', lambda e: e.memset(pb[0][:], 0.0), writes=['pb0'])
        P.op('dve', lambda e: e.memset(pb[1][:], 0.0), writes=['pb1'])
        P.op('act', ACT(svf[:], par[:, P_CV:P_CV + 32], AF.Silu), reads=['par'], writes=['svf'])
        P.op('dve', CP(svec[:].rearrange("p k v -> p (k v)"), svf[:]), reads=['svf'], writes=['svec'])
        P.op('act', ACT(es[:], par[:, P_SINK:P_SINK + 16], AF.Exp), reads=['par'], writes=['es'])
        P.op('dve', TT(lprod[:, 0:2, :], par[:, P_LQ1:P_LQ1 + 128].rearrange("p (l i) -> p l i", l=2),
                       par[:, P_LK1:P_LK1 + 128].rearrange("p (l i) -> p l i", l=2), ALU.mult), reads=['par'], writes=['lprodA'])
        P.op('dve', TT(lprod[:, 2:4, :], par[:, P_LQ2:P_LQ2 + 128].rearrange("p (l i) -> p l i", l=2),
                       par[:, P_LK2:P_LK2 + 128].rearrange("p (l i) -> p l i", l=2), ALU.mult), reads=['par'], writes=['lprodB'])
        P.op('dve', lambda e: e.reduce_sum(out=lamt[:, 0:4], in_=lprod[:], axis=mybir.AxisListType.X),
             reads=['lprodA', 'lprodB'], writes=['lamt'])
        P.op('act', ACT(lamt[:, 4:8], lamt[:, 0:4], AF.Exp), reads=['lamt'], writes=['lamt2'])
        for l in range(NL):
            P.op('dve', STT(lamt[:, 8 + l:9 + l], lamt[:, 6 + l:7 + l], -LAM_INIT[l], lamt[:, 4 + l:5 + l], ALU.add, ALU.subtract),
                 reads=['lamt2'], writes=[('nlam', l)])
            P.op('dve', TS(lamt[:, 10 + l:11 + l], par[:, P_SUB + l:P_SUB + l + 1], 1.0 - LAM_INIT[l], 0.0, ALU.mult, ALU.add),
                 reads=['par'], writes=[('gsub', l)])

        wstate = {'i': 0}

        def wload(src, ksub, ncol):
            i = wstate['i'] % NWB
            wstate['i'] += 1
            key = f'wb{i}'
            P.dma('pool', [DMA(wbufs[i][:, 0:ksub, 0:ncol], src.rearrange("(k p) c -> p k c", p=128))], writes=[key], semkey='D_' + key)
            return wbufs[i], key

        def fm_tile(W2d, col0, ncol, K, rhs_fn, rhs_reads, banks, evac, N=512):
            nkc = K // 128
            nf = ncol // 128
            for ks in range(nkc // KSUB):
                buf, key = wload(W2d[ks * KSUB * 128:(ks + 1) * KSUB * 128, col0:col0 + ncol], KSUB, ncol)
                fns = []
                for j in range(nf):
                    for kk in range(KSUB):
                        kc = ks * KSUB + kk
                        fns.append(MM(bank(banks[j], N), buf[:, kk, j * 128:(j + 1) * 128], rhs_fn(kc), kc == 0, kc == nkc - 1))
                P.mm(fns, reads=[key] + rhs_reads, writes=[psk(banks[j]) for j in range(nf)])
            for j in range(nf):
                evac(j, banks[j])

        chk('const')
        for l in range(NL):
            P.op('dve', lambda e, l=l: e.memset(ps[:, l, 0:192], 0.0), writes=[psk(l)])
            for ct in range(24):
                for ks in range(2):
                    buf, key = wload(d_wada[l, ks * 1024:(ks + 1) * 1024, ct * 512:(ct + 1) * 512], KSUB, 512)
                    fns = []
                    for j in range(4):
                        fc = ct * 4 + j
                        for kk in range(KSUB):
                            kc = ks * KSUB + kk
                            fns.append(lambda e, o=ps[:, l, fc * 2:fc * 2 + 2], w=buf[:, kk, j * 128:(j + 1) * 128], r=svec[:, kc, :], sp=(kc == NKC - 1):
                                       e.matmul(o, lhsT=w, rhs=r, start=False, stop=sp, skip_group_check=True))
                    P.mm(fns, reads=[key, 'svec'], writes=[psk(l)])
            for v in range(2):
                P.op('dve', TT(mod[:, l, v, :], ps[:, l, 0:192].rearrange("p (f v) -> p f v", v=2)[:, :, v],
                               par[:, P_BA + l * 96:P_BA + (l + 1) * 96], ALU.add), reads=[psk(l), 'par'], writes=[('mod', l, v)])
                P.op('dve', STT(AB[:, l, v, 0, :], mod[:, l, v, 16:32], 1.0, par[:, P_NMIX + l * 16:P_NMIX + (l + 1) * 16], ALU.add, ALU.mult),
                     reads=[('mod', l, v), 'par'], writes=[('AB', l, v)])
                P.op('dve', STT(AB[:, l, v, 1, :], mod[:, l, v, 64:80], 1.0, par[:, P_NMLP + l * 16:P_NMLP + (l + 1) * 16], ALU.add, ALU.mult),
                     reads=[('mod', l, v), 'par'], writes=[('AB', l, v)])

        tmp_rot = {'i': 0}

        def nexttmp():
            k = ("tA", "tB")[tmp_rot['i'] % 2]
            tmp_rot['i'] += 1
            return tmp[k], k

        def norm_stats(g):
            for kc in range(NKC):
                P.op('dve', TT(hT[:, kc, :], xT[:, g, kc, :], xT[:, g, kc, :], ALU.mult), reads=[('x', g, kc)], writes=[('h', kc)])
            P.mm([MM(bank(7), ones[:], hT[:, kc, :], kc == 0, kc == NKC - 1) for kc in range(NKC)], reads=H + ['ones'], writes=[psk(7)])
            P.op('act', ACT(rstd[:], bank(7), AF.Ln, scale=1.0 / D_MODEL, bias=epst[:, 0:1]), reads=[psk(7), 'eps'], writes=['rstd'])
            P.op('act', ACT(rstd[:], rstd[:], AF.Exp, scale=-0.5), reads=['rstd'], writes=['rstd'])

        def norm_to_h(l, g, which):
            norm_stats(g)
            boff = 0 if which == 0 else 48
            for kc in range(NKC):
                t, tk = nexttmp()
                P.op('dve', STT(t[:], xT[:, g, kc, :], AB[:, l, g, which, kc:kc + 1], rstd[:], ALU.mult, ALU.mult),
                     reads=[('x', g, kc), 'rstd', ('AB', l, g)], writes=[tk])
                P.op('act', ACT(hT[:, kc, :], t[:], AF.Identity, bias=mod[:, l, g, boff + kc:boff + kc + 1]),
                     reads=[tk, ('mod', l, g)], writes=[('h', kc)])

        rot = {'xs': 0, 'xb': 0, 'E': 0, 'S': 0, 'it': 0, 'kv': 0}

        def rope_evac(b, dest, dkey, ci, pi):
            i = rot['xb'] % 2
            rot['xb'] += 1
            xsb = 4 + rot['xs'] % 4
            rot['xs'] += 1
            P.op('act', ACT(xb[i][:], bank(b), AF.Copy), reads=[psk(b)], writes=[('xb', i)])
            P.mm([MM(bank(xsb), perm[:, pi, :], xb[i][:], True, True)], reads=[('xb', i), 'cmask'], writes=[psk(xsb)])
            P.op('dve', TT(tmp['tC'][:], bank(b), rope[:, ci, :], ALU.mult), reads=[psk(b), 'rope'], writes=['tC'])
            P.op('dve', TT(tmp['tD'][:], bank(xsb), rope[:, ci + 1, :], ALU.mult), reads=[psk(xsb), 'rope'], writes=['tD'])
            P.op('dve', TT(dest, tmp['tC'][:], tmp['tD'][:], ALU.add), reads=['tC', 'tD'], writes=[dkey])

        def stage_out(b, n, dst, src_is_bank=True):
            t, tk = nexttmp()
            P.op('dve', CP(t[:, 0:n], bank(b, n)), reads=[psk(b)], writes=[tk])
            P.dma('sp', [DMA(dst, t[:, 0:n])], reads=[tk], writes=['OUT'], out=True)

        hrhs = lambda kc: hT[:, kc, :]

        def run_pass(l, g):
            Win = d_win[l]
            norm_to_h(l, g, 0)
            for half in range(2):
                def ev_q(j, b, half=half):
                    h = half * 4 + j
                    if g == 0:
                        P.op('act', ACT(qw[:, h, :], bank(b), AF.Copy), reads=[psk(b)], writes=[R0a])
                    else:
                        rope_evac(b, qw[:, h, :], R0a, 0, 0)
                fm_tile(Win, WQ + half * 512, 512, D_MODEL, hrhs, H, [0, 1, 2, 3], ev_q)
            chk(f'wq{l}{g}')
            for ks in range(2):
                buf, key = wload(Win[ks * 1024:(ks + 1) * 1024, WK:WK + 512], KSUB, 512)
                fns = []
                for j in range(2):
                    for kk in range(KSUB):
                        kc = ks * KSUB + kk
                        fns.append(MM(bank(j), buf[:, kk, j * 128:(j + 1) * 128], hT[:, kc, :], kc == 0, kc == NKC - 1))
                for tt in range(4):
                    for kk in range(KSUB):
                        kc = ks * KSUB + kk
                        fns.append(MM(bank(2 + tt, 256), hT[:, kc, tt * 128:(tt + 1) * 128], buf[:, kk, 256:512], kc == 0, kc == NKC - 1))
                P.mm(fns, reads=[key] + H, writes=[psk(b) for b in range(6)])
            chk(f'wkvmm{l}{g}')
            for tt in range(4):
                P.op('act', ACT(vw_[:, tt, :], bank(2 + tt, 256), AF.Copy), reads=[psk(2 + tt)], writes=[R0b])
                if g == 0:
                    stage_out(2 + tt, 256, o_vw[l, tt * 128:(tt + 1) * 128, :])
            chk(f'kwev{l}{g}')
            for j in range(2):
                if g == 0:
                    P.op('act', ACT(kw_[:, j, :], bank(j), AF.Copy), reads=[psk(j)], writes=[R0b])
                    stage_out(j, 512, o_kw[:, l, j, :])
                else:
                    rope_evac(j, kw_[:, j, :], R0b, 0, 0)
            if g == 1:
                sw = sendW[l].ap()
                kview = lambda r0: sw[r0:r0 + 64, :].rearrange("r (a c) -> (r a) c", c=128)
                vview = lambda r0: sw[r0:r0 + 64, :].rearrange("r (a c) -> (r a) c", c=256)
                fns = []
                for hk in range(2):
                    fns.append(DMA(kview(0)[hk * 128:(hk + 1) * 128, :], kw_[:, hk, 0:128]))
                    fns.append(DMA(kview(64)[hk * 128:(hk + 1) * 128, :], kw_[:, hk, 384:512]))
                fns.append(DMA(vview(128), vw_[:, 0, :]))
                fns.append(DMA(vview(192), vw_[:, 3, :]))
                P.dma('sp', fns, reads=[R0b], writes=[('sendW', l)])
                if _DEBUG_NOCC:
                    P.dma('pool', [DMA(gathW[l].ap()[0:NRW, :], sendW[l].ap())], reads=[('sendW', l)], writes=[('gathW', l)], semkey=f'CCW{l}')
                else:
                    P.dma('pool', [lambda e: e.collective_compute("AllGather", ALU.bypass, replica_groups=(_GROUPS_OVERRIDE or GROUPS),
                                                                  ins=[sendW[l].ap()], outs=[gathW[l].ap()])],
                          reads=[('sendW', l)], writes=[('gathW', l)], semkey=f'CCW{l}', inc=1)
            chk(f'winproj{l}{g}')
            pv = pb[g]
            ns, nt = (2, 256) if g == 0 else (1, 512)
            for half in range(2):
                def ev_cb(j, b):
                    P.op('act', ACT(cbS[:, j, :], bank(b), AF.Copy), reads=[psk(b)], writes=[YB])

                def ev_cc(j, b):
                    P.op('act', ACT(ccS[:, j, :], bank(b), AF.Copy), reads=[psk(b)], writes=[YC])

                def ev_cx(j, b, half=half):
                    i = half * 4 + j
                    cw = lambda tap: pcol(P_CW + (l * 3 + tap) * 8 + i)
                    v3 = lambda ap: ap.rearrange("p (s t) -> p s t", s=ns)
                    P.op('dve', TT(pv[:, :, 1:1 + nt], v3(bank(b)), v3(ccS[:, j, :]), ALU.mult), reads=[psk(b), YC], writes=[('pb', g)])
                    tC, tD = tmp['tC'][:], tmp['tD'][:]
                    P.op('dve', TSM(v3(tC), pv[:, :, 1:1 + nt], cw(1)), reads=[('pb', g), 'par'], writes=['tC'])
                    P.op('dve', STT(v3(tD), pv[:, :, 0:nt], cw(0), v3(tC), ALU.mult, ALU.add), reads=[('pb', g), 'tC', 'par'], writes=['tD'])
                    P.op('dve', STT(v3(tC), pv[:, :, 2:2 + nt], cw(2), v3(tD), ALU.mult, ALU.add), reads=[('pb', g), 'tD', 'par'], writes=['tC'])
                    P.op('dve', TT(y_a[:, i, :], tC, cbS[:, j, :], ALU.mult), reads=['tC', YB], writes=[YA])
                    if g == 1:
                        P.op('dve', CP(phs[:, i:i + 1], pv[:, 0, 1:2]), reads=[('pb', g)], writes=['phs'])
                        P.op('dve', CP(phs[:, 8 + i:9 + i], pv[:, 0, 512:513]), reads=[('pb', g)], writes=['phs'])
                        P.op('dve', CP(cbb[:, i, 0:1], cbS[:, j, 0:1]), reads=[YB], writes=['cbb'])
                        P.op('dve', CP(cbb[:, i, 1:2], cbS[:, j, 511:512]), reads=[YB], writes=['cbb'])
                fm_tile(Win, CB + half * 512, 512, D_MODEL, hrhs, H, [0, 1, 2, 3], ev_cb)
                fm_tile(Win, CC + half * 512, 512, D_MODEL, hrhs, H, [4, 5, 6, 7], ev_cc)
                fm_tile(Win, CX + half * 512, 512, D_MODEL, hrhs, H, [0, 1, 2, 3], ev_cx)
            chk(f'conv{l}{g}')
            sc_w = 128 ** -0.5
            if g == 1:
                gw = gathW[l].ap().rearrange("(r n) c -> r n c", r=4)
                P.dma('pool', [DMA(kcw[:, :, :], d_ckw[l]),
                               DMA(vcw[:, :, :], d_cvw[l].rearrange("(c p) f -> p c f", p=128))], writes=[R0b], semkey='D_cw')
                fns = []
                for part, r0 in ((0, 64), (1, 0)):
                    kv4 = gw[:, r0:r0 + 64, :].rearrange("r n (a c) -> r (n a) c", c=128)
                    for hk in range(2):
                        fns.append(DMA(khalo[:, hk, part * 4:(part + 1) * 4, :], kv4[:, hk * 128:(hk + 1) * 128, :].rearrange("r d c -> d r c")))
                for part, r0 in ((0, 192), (1, 128)):
                    vv4 = gw[:, r0:r0 + 64, :].rearrange("r n (a c) -> r (n a) c", c=256)
                    fns.append(DMA(vhalo[:, part * 4:(part + 1) * 4, :], vv4.rearrange("r t c -> t r c")))
                P.dma('sp', fns, reads=[('gathW', l)], writes=[R0c])

            def win_block(hk, qsl, ysl, chunks):
                it = rot['it']
                rot['it'] += 1
                ob, db = 2 + it % 2, 4 + it % 2
                rhs_q = qw[:, 4 * hk:4 * hk + 4, qsl]
                n = len(chunks)
                Es = {}

                def qk(ci):
                    k_ap, v_ap, m_ap, s_ap, rd = chunks[ci]
                    sb_ = rot['S'] % 2
                    rot['S'] += 1
                    ei = rot['E'] % 3
                    rot['E'] += 1
                    E = Eb[ei][:, 0, :]
                    Es[ci] = (E, ei)
                    P.mm([MM(bank(sb_), k_ap, rhs_q, True, True)], reads=[R0a] + rd, writes=[psk(sb_)])
                    P.op('act', ACT(E, bank(sb_), AF.Exp, scale=sc_w), reads=[psk(sb_)], writes=[('E', ei)])
                    if m_ap is not None:
                        if s_ap is None:
                            P.op('dve', TT(E, E, m_ap, ALU.mult), reads=[('E', ei), 'cmask'], writes=[('E', ei)])
                        else:
                            P.op('dve', STT(E, E, s_ap, m_ap, ALU.mult, ALU.mult), reads=[('E', ei), 'cmask', 'par'], writes=[('E', ei)])

                def pv(ci):
                    k_ap, v_ap, m_ap, s_ap, rd = chunks[ci]
                    E, ei = Es[ci]
                    P.mm([MM(bank(ob), v_ap, E, ci == 0, ci == n - 1), MM(bank(db), ones[:], E, ci == 0, ci == n - 1)],
                         reads=[('E', ei), 'ones'] + rd, writes=[psk(ob), psk(db)])
                qk(0)
                for ci in range(n):
                    if ci + 1 < n:
                        qk(ci + 1)
                    pv(ci)
                tC = tmp['tC'][:]
                for gg in range(4):
                    h = 4 * hk + gg
                    P.op('dve', TSA(tC[:, gg * 128:(gg + 1) * 128], ps[:, db, gg * 128:(gg + 1) * 128], es[:, l * 8 + h:l * 8 + h + 1]),
                         reads=[psk(db), 'es'], writes=['tC'])
                P.op('dve', RCP(tC, tC), reads=['tC'], writes=['tC'])
                P.op('dve', TT(y_b[:, 4 * hk:4 * hk + 4, ysl], bank(ob).rearrange("p (g q) -> p g q", g=4),
                               tC.rearrange("p (g q) -> p g q", g=4), ALU.mult), reads=[psk(ob), 'tC'], writes=[YB])

            mL, mR = cmask[:, 0, :], cmask[:, 1, :]
            if g == 0:
                for s in range(2):
                    for hk in range(2):
                        for qb in range(2):
                            sl = slice(s * 256 + qb * 128, s * 256 + qb * 128 + 128)
                            chunks = [(kw_[:, hk, s * 256 + c * 128:s * 256 + (c + 1) * 128], vw_[:, s * 2 + c, hk * 128:(hk + 1) * 128], None, None, [R0b])
                                      for c in range(2)]
                            win_block(hk, sl, sl, chunks)
            else:
                for hk in range(2):
                    for qb in range(4):
                        sl = slice(qb * 128, (qb + 1) * 128)
                        chunks = [(kcw[:, hk, c * 128:(c + 1) * 128], vcw[:, c, hk * 128:(hk + 1) * 128], None, None, [R0b]) for c in range(4)]
                        for cc in (qb - 1, qb, qb + 1):
                            if 0 <= cc <= 3:
                                m = mL if cc == qb - 1 else (mR if cc == qb + 1 else None)
                                chunks.append((kw_[:, hk, cc * 128:(cc + 1) * 128], vw_[:, cc, hk * 128:(hk + 1) * 128], m, None, [R0b]))
                        if qb == 0:
                            for r in range(4):
                                chunks.append((khalo[:, hk, r, :], vhalo[:, r, hk * 128:(hk + 1) * 128], mL, pcol(P_SELL + r), [R0c]))
                        if qb == 3:
                            for r in range(4):
                                chunks.append((khalo[:, hk, 4 + r, :], vhalo[:, 4 + r, hk * 128:(hk + 1) * 128], mR, pcol(P_SELR + r), [R0c]))
                        win_block(hk, sl, sl, chunks)
            chk(f'winattn{l}{g}')
            for half in range(2):
                def ev_dq(j, b, half=half):
                    h = half * 4 + j
                    if g == 0:
                        P.op('act', ACT(qd[:, h, :], bank(b), AF.Copy), reads=[psk(b)], writes=[R0a])
                    else:
                        rope_evac(b, qd[:, h, :], R0a, 2, 1)
                fm_tile(Win, DQ + half * 512, 512, D_MODEL, hrhs, H, [0, 1, 2, 3], ev_dq)
            for half in range(2):
                def ev_dk(j, b, half=half):
                    h = half * 4 + j
                    if g == 0:
                        P.op('act', ACT(kd[:, h, :], bank(b), AF.Copy), reads=[psk(b)], writes=[R0b])
                        stage_out(b, 512, o_kd[:, l, h, :])
                    else:
                        rope_evac(b, kd[:, h, :], R0b, 2, 1)
                fm_tile(Win, DK + half * 512, 512, D_MODEL, hrhs, H, [0, 1, 2, 3], ev_dk)
            for half in range(2):
                banks = [0, 1, 2, 3] if half == 0 else [4, 5, 6, 7]
                for ks in range(2):
                    buf, key = wload(Win[ks * 1024:(ks + 1) * 1024, DV + half * 512:DV + (half + 1) * 512], KSUB, 512)
                    fns = []
                    for tt in range(4):
                        for kk in range(KSUB):
                            kc = ks * KSUB + kk
                            fns.append(MM(bank(banks[tt]), hT[:, kc, tt * 128:(tt + 1) * 128], buf[:, kk, :], kc == 0, kc == NKC - 1))
                    P.mm(fns, reads=[key] + H, writes=[psk(b) for b in banks])
                for tt in range(4):
                    P.op('act', ACT(vd_[:, tt, half * 512:(half + 1) * 512], bank(banks[tt]), AF.Copy), reads=[psk(banks[tt])], writes=[R0c])
                    if g == 0:
                        stage_out(banks[tt], 512, o_vd[l, tt * 128:(tt + 1) * 128, half * 512:(half + 1) * 512])
            if g == 1:
                for nm, snd, gth, src_fn, rd in (
                        ('K', sendK[l], gathK[l], lambda: DMA(sendK[l].ap().rearrange("(h d) t -> d h t", d=128), kd[:, :, :]), [R0b]),
                        ('V', sendV[l], gathV[l], lambda: DMA(sendV[l].ap().rearrange("(t two) c -> t (two c)", two=2).rearrange("(c p) f -> p c f", p=128), vd_[:, :, :]), [R0c]),
                        ('P', sendP[l], gathP[l], lambda: DMA(sendP[l].ap()[0:4, :].rearrange("r (a c) -> (r a) c", c=16), phs[:]), ['phs'])):
                    P.dma('sp', [src_fn()], reads=rd, writes=[('send' + nm, l)])
                    if _DEBUG_NOCC:
                        P.dma('pool', [DMA(gth.ap()[0:snd.ap().shape[0], :], snd.ap())], reads=[('send' + nm, l)], writes=[('gath' + nm, l)], semkey=f'CC{nm}{l}')
                    else:
                        P.dma('pool', [lambda e, snd=snd, gth=gth: e.collective_compute("AllGather", ALU.bypass, replica_groups=(_GROUPS_OVERRIDE or GROUPS),
                                                                                        ins=[snd.ap()], outs=[gth.ap()])],
                              reads=[('send' + nm, l)], writes=[('gath' + nm, l)], semkey=f'CC{nm}{l}', inc=1)
            chk(f'diffproj{l}{g}')
            sc_d = 64 ** -0.5
            nlam = lamt[:, 8 + l:9 + l]
            gsub = lamt[:, 10 + l:11 + l]

            def diff_finish(h, tsl, n, t0, t1):
                tD = tmp['tD'][:, 0:n]
                P.op('dve', STT(tD, t1, nlam, t0, ALU.mult, ALU.add), reads=['tA', 'tB', ('nlam', l)], writes=['tD'])
                P.op('dve', TT(sqb[:, 0:n], tD, tD, ALU.mult), reads=['tD'], writes=['sqb'])
                P.mm([MM(bank(0, n), ones[:], sqb[:, 0:n], True, True)], reads=['sqb', 'ones'], writes=[psk(0)])
                tC = tmp['tC'][:, 0:n]
                P.op('act', ACT(tC, bank(0, n), AF.Ln, scale=1.0 / 128, bias=epst[:, 0:1]), reads=[psk(0), 'eps'], writes=['tC'])
                P.op('act', ACT(tC, tC, AF.Exp, scale=-0.5), reads=['tC'], writes=['tC'])
                P.op('dve', STT(y_c[:, h, tsl], tD, gsub, tC, ALU.mult, ALU.mult), reads=['tD', 'tC', ('gsub', l)], writes=[YC])

            if g == 0:
                for s in range(2):
                    for h in range(8):
                        it = rot['it']
                        rot['it'] += 1
                        ob, db = 4 + it % 2, 6 + it % 2
                        tsl = slice(s * 256, (s + 1) * 256)
                        Es = {}

                        def qk(c, s=s, h=h, tsl=tsl):
                            sb_ = 2 * (rot['S'] % 2)
                            rot['S'] += 1
                            ei = rot['E'] % 3
                            rot['E'] += 1
                            E = Eb[ei][:, :, 0:256]
                            Es[c] = (E, ei)
                            ksl = slice(s * 256 + c * 128, s * 256 + (c + 1) * 128)
                            P.mm([MM(ps[:, sb_, 0:256], kd[0:64, h, ksl], qd[0:64, h, tsl], True, True),
                                  MM(ps[:, sb_ + 1, 0:256], kd[64:128, h, ksl], qd[64:128, h, tsl], True, True)],
                                 reads=[R0a, R0b], writes=[psk(sb_), psk(sb_ + 1)])
                            P.op('act', ACT(E, ps[:, sb_:sb_ + 2, 0:256], AF.Exp, scale=sc_d), reads=[psk(sb_), psk(sb_ + 1)], writes=[('E', ei)])

                        def pv(c, s=s, h=h, ob=ob, db=db):
                            E, ei = Es[c]
                            P.mm([MM(bank(ob), vd_[:, s * 2 + c, h * 128:(h + 1) * 128], E, c == 0, c == 1),
                                  MM(bank(db), ones[:], E, c == 0, c == 1)], reads=[('E', ei), 'ones', R0c], writes=[psk(ob), psk(db)])
                        qk(0)
                        qk(1)
                        pv(0)
                        pv(1)
                        tA, tB = tmp['tA'][:], tmp['tB'][:]
                        P.op('dve', RCP(tB, bank(db)), reads=[psk(db)], writes=['tB'])
                        P.op('dve', TT(tA, bank(ob), tB, ALU.mult), reads=[psk(ob), 'tB'], writes=['tA'])
                        diff_finish(h, tsl, 256, tA[:, 0:256], tA[:, 256:512])
            else:
                gd = gathK[l].ap().rearrange("(r n) c -> n r c", r=4)
                for h in range(8):
                    pieces = []
                    for pc in range(3):
                        si = rot['kv'] % 3
                        rot['kv'] += 1
                        kb, vb = kvk[si], kvv[si]
                        if pc == 0:
                            P.dma('pool', [DMA(kb[:, 0:512], d_ckd[l, h]),
                                           DMA(vb[:, 0:4, :], d_cvd[l].rearrange("(c p) f -> p c f", p=128)[:, :, h * 128:(h + 1) * 128])],
                                  writes=[('kv', si)], semkey=f'D_kv{si}')
                            nch = 4
                        else:
                            r0 = (pc - 1) * 2
                            fns = [DMA(kb[:, :].rearrange("p (r t) -> p r t", r=2), gd[h * 128:(h + 1) * 128, r0:r0 + 2, :])]
                            for rr in range(2):
                                vsrc = gathV[l].ap()[(r0 + rr) * 1024:(r0 + rr + 1) * 1024, :] \
                                    .rearrange("(t two) c -> t (two c)", two=2).rearrange("(c p) f -> p c f", p=128)
                                fns.append(DMA(vb[:, rr * 4:(rr + 1) * 4, :], vsrc[:, :, h * 128:(h + 1) * 128]))
                            P.dma('sp', fns, reads=[('gathK', l), ('gathV', l)], writes=[('kv', si)], semkey=f'D_kv{si}')
                            nch = 8
                        pieces.append((si, nch))
                    tot = 20
                    clist = [(si, c) for si, nch in pieces for c in range(nch)]
                    Es = {}

                    def qk(ci, h=h):
                        si, c = clist[ci]
                        kb = kvk[si]
                        sb_ = 2 * (rot['S'] % 2)
                        rot['S'] += 1
                        ei = rot['E'] % 3
                        rot['E'] += 1
                        Es[ci] = ei
                        P.mm([MM(bank(sb_), kb[0:64, c * 128:(c + 1) * 128], qd[0:64, h, :], True, True),
                              MM(bank(sb_ + 1), kb[64:128, c * 128:(c + 1) * 128], qd[64:128, h, :], True, True)],
                             reads=[R0a, ('kv', si)], writes=[psk(sb_), psk(sb_ + 1)])
                        P.op('act', ACT(Eb[ei][:], ps[:, sb_:sb_ + 2, :], AF.Exp, scale=sc_d), reads=[psk(sb_), psk(sb_ + 1)], writes=[('E', ei)])

                    def pv(ci):
                        si, c = clist[ci]
                        vb = kvv[si]
                        ei = Es[ci]
                        st, sp_ = ci == 0, ci == tot - 1
                        P.mm([MM(bank(4), vb[:, c, :], Eb[ei][:, 0, :], st, sp_), MM(bank(5), vb[:, c, :], Eb[ei][:, 1, :], st, sp_),
                              MM(bank(6), ones[:], Eb[ei][:, 0, :], st, sp_), MM(bank(7), ones[:], Eb[ei][:, 1, :], st, sp_)],
                             reads=[('E', ei), 'ones', ('kv', si)], writes=[psk(4), psk(5), psk(6), psk(7)])
                    qk(0)
                    for ci in range(tot):
                        if ci + 1 < tot:
                            qk(ci + 1)
                        pv(ci)
                    tA, tB, tC = tmp['tA'][:], tmp['tB'][:], tmp['tC'][:]
                    P.op('dve', RCP(tC, bank(6)), reads=[psk(6)], writes=['tC'])
                    P.op('dve', TT(tA, bank(4), tC, ALU.mult), reads=[psk(4), 'tC'], writes=['tA'])
                    P.op('dve', RCP(tC, bank(7)), reads=[psk(7)], writes=['tC'])
                    P.op('dve', TT(tB, bank(5), tC, ALU.mult), reads=[psk(5), 'tC'], writes=['tB'])
                    diff_finish(h, slice(0, 512), 512, tA, tB)
                P.dma('sp', [DMA(pha[:, r, :], gathP[l].ap()[r * 8:r * 8 + 4, :].rearrange("r (a c) -> (r a) c", c=16)) for r in range(4)],
                      reads=[('gathP', l)], writes=['pha'])
                for side, selc, src0 in ((0, P_SELL, 8), (1, P_SELR, 0)):
                    acc = fix[:, side, :]
                    P.op('dve', TSM(acc, pha[:, 0, src0:src0 + 8], pcol(selc)), reads=['pha', 'par'], writes=[('fix', side)])
                    for r in range(1, 4):
                        P.op('dve', STT(acc, pha[:, r, src0:src0 + 8], pcol(selc + r), acc, ALU.mult, ALU.add), reads=['pha', 'par', ('fix', side)], writes=[('fix', side)])
                    tap = 0 if side == 0 else 2
                    cwv = par[:, P_CW + (l * 3 + tap) * 8:P_CW + (l * 3 + tap) * 8 + 8]
                    d2 = fix[:, 2 + side, :]
                    P.op('dve', TT(d2, acc, cwv, ALU.mult), reads=[('fix', side), 'par'], writes=[('fix', 2 + side)])
                    P.op('dve', TT(d2, d2, cbb[:, :, side], ALU.mult), reads=[('fix', 2 + side), 'cbb'], writes=[('fix', 2 + side)])
                    col = 0 if side == 0 else 511
                    P.op('dve', TT(y_a[:, :, col], y_a[:, :, col], d2, ALU.add), reads=[YA, ('fix', 2 + side)], writes=[YA])
            chk(f'diffattn{l}{g}')
            for br, (ysrc, ykey, gcol) in enumerate(((y_a, YA, GA), (y_b, YB, GB), (y_c, YC, GC))):
                for ct in range(4):
                    def ev_gate(j, b):
                        P.op('act', ACT(Eb[j // 2][:, j % 2, :], bank(b), AF.Sigmoid), reads=[psk(b)], writes=[('E', j // 2)])
                    fm_tile(Win, gcol + ct * 512, 512, D_MODEL, hrhs, H, [0, 1, 2, 3], ev_gate)

                    def ev_br(j, b, ct=ct, br=br):
                        fc = ct * 4 + j
                        gt = Eb[j // 2][:, j % 2, :]
                        if br == 0:
                            P.op('dve', TT(merged[:, fc, :], bank(b), gt, ALU.mult), reads=[psk(b), ('E', j // 2)], writes=[R0a, R0b])
                        else:
                            tC = tmp['tC'][:]
                            P.op('dve', TT(tC, bank(b), gt, ALU.mult), reads=[psk(b), ('E', j // 2)], writes=['tC'])
                            P.op('dve', TT(merged[:, fc, :], merged[:, fc, :], tC, ALU.add), reads=['tC', R0a, R0b], writes=[R0a, R0b])
                    fm_tile(d_wbr[br][l], ct * 512, 512, 1024, lambda kc, ysrc=ysrc: ysrc[:, kc, :], [ykey], [4, 5, 6, 7], ev_br)
            chk(f'branch{l}{g}')
            for ct in range(4):
                def ev_o(j, b, ct=ct):
                    fc = ct * 4 + j
                    P.op('dve', STT(xT[:, g, fc, :], bank(b), mod[:, l, g, 32 + fc:33 + fc], xT[:, g, fc, :], ALU.mult, ALU.add),
                         reads=[psk(b), ('mod', l, g), ('x', g, fc)], writes=[('x', g, fc)])
                fm_tile(d_wout[l], ct * 512, 512, D_MODEL, lambda kc: merged[:, kc, :], [R0a, R0b], [0, 1, 2, 3] if ct % 2 == 0 else [4, 5, 6, 7], ev_o)
            chk(f'wout{l}{g}')
            norm_to_h(l, g, 1)
            tile_i = 0
            for half in range(2):
                for ct in range(8):
                    def ev_m1(j, b, ct=ct):
                        t, tk = nexttmp()
                        P.op('act', ACT(t[:], bank(b), AF.Relu), reads=[psk(b)], writes=[tk])
                        P.op('dve', TT(act[:, ct * 4 + j, :], t[:], t[:], ALU.mult), reads=[tk], writes=[R0a, R0b, R0c, YA])
                    fm_tile(d_w1[l], half * 4096 + ct * 512, 512, D_MODEL, hrhs, H, [0, 1, 2, 3] if tile_i % 2 == 0 else [4, 5, 6, 7], ev_m1)
                    tile_i += 1
                for ct in range(4):
                    def ev_m2(j, b, ct=ct):
                        fc = ct * 4 + j
                        P.op('dve', STT(xT[:, g, fc, :], bank(b), mod[:, l, g, 80 + fc:81 + fc], xT[:, g, fc, :], ALU.mult, ALU.add),
                             reads=[psk(b), ('mod', l, g), ('x', g, fc)], writes=[('x', g, fc)])
                    fm_tile(d_w2[l][half * 4096:(half + 1) * 4096, :], ct * 512, 512, 4096, lambda kc: act[:, kc, :], [R0a, R0b, R0c, YA],
                            [0, 1, 2, 3] if tile_i % 2 == 0 else [4, 5, 6, 7], ev_m2)
                    tile_i += 1

        def final_norm(g):
            norm_stats(g)
            for kc in range(NKC):
                t, tk = nexttmp()
                P.op('dve', STT(t[:], xT[:, g, kc, :], par[:, P_NFIN + kc:P_NFIN + kc + 1], rstd[:], ALU.mult, ALU.mult),
                     reads=[('x', g, kc), 'rstd', 'par'], writes=[tk])
                P.dma('sp', [DMA(o_y[:, g, kc, :], t[:])], reads=[tk], writes=['OUT'], out=True)

        chk('p0')
        for l in range(NL):
            for g in range(2):
                run_pass(l, g)
                chk(f'pass{l}{g}')
                if l == NL - 1:
                    final_norm(g)

    try:
        emit()
    except _Stop:
        pass
    P.finish()
    return nc


_NC_CACHE = {}
_DEBUG_STOP = None
_DEBUG_CORES = 8


def _rope_tables(j):
    t = np.arange(512 * j, 512 * j + 512)
    row = (t // 64).astype(np.float32)
    col = (t % 64).astype(np.float32)
    out = np.zeros((128, 4, 512), np.float32)
    inv32 = (10000.0 ** (-np.arange(32, dtype=np.float32) / 32)).astype(np.float32)
    inv16 = (10000.0 ** (-np.arange(16, dtype=np.float32) / 16)).astype(np.float32)
    for d in range(128):
        blk = d // 32
        pos = row if blk < 2 else col
        ang = (pos * inv32[d % 32]).astype(np.float32)
        out[d, 0] = np.cos(ang)
        out[d, 1] = -np.sin(ang) if blk % 2 == 0 else np.sin(ang)
        e = d % 64
        blk = e // 16
        pos = row if blk < 2 else col
        ang = (pos * inv16[e % 16]).astype(np.float32)
        out[d, 2] = np.cos(ang)
        out[d, 3] = -np.sin(ang) if blk % 2 == 0 else np.sin(ang)
    return out


def _const_tables():
    perm = np.zeros((128, 2, 128), np.float32)
    for d in range(128):
        pw = d + 32 if (d % 64) < 32 else d - 32
        pd = d + 16 if (d % 32) < 16 else d - 16
        perm[pw, 0, d] = 1.0
        perm[pd, 1, d] = 1.0
    i = np.arange(128)[:, None]
    q = np.arange(128)[None, :]
    mL = (i >= q).astype(np.float32)
    mR = (i <= q).astype(np.float32)
    cmask = np.stack([np.tile(mL, (1, 4)), np.tile(mR, (1, 4))], axis=1)
    return perm.astype(ml_dtypes.bfloat16), cmask.astype(ml_dtypes.bfloat16)


def kernel(x_prompt, x_sample, cache_win_k, cache_win_v, cache_diff_k, cache_diff_v, c, c_ctx,
           w_ada, b_ada, norm_mix, norm_mlp, w_in, conv_w, win_sink,
           lambda_q1, lambda_k1, lambda_q2, lambda_k2, diff_subln,
           w_branch_conv, w_branch_win, w_branch_diff, w_out, w_mlp1, w_mlp2, norm_final):
    f32 = lambda a: np.ascontiguousarray(np.asarray(a, dtype=np.float32))
    x_prompt, x_sample = f32(x_prompt), f32(x_sample)
    if 'nc' not in _NC_CACHE:
        _NC_CACHE['nc'] = build_program(_DEBUG_STOP)
    nc = _NC_CACHE['nc']
    perm, cmask = _const_tables()
    shared = {"w_ada": f32(w_ada), "w_in": f32(w_in), "w_branch_conv": f32(w_branch_conv), "w_branch_win": f32(w_branch_win),
              "w_branch_diff": f32(w_branch_diff), "w_out": f32(w_out), "w_mlp1": f32(w_mlp1), "w_mlp2": f32(w_mlp2),
              "perm": perm, "cmask": cmask}
    fm = lambda v: f32(v).reshape(-1, 128).T
    in_maps = []
    for core in range(8):
        s, j = core // 4, core % 4
        xp = x_prompt[2 * core:2 * core + 2].reshape(512, D_MODEL)
        xs = x_sample[s, 512 * j:512 * (j + 1)]
        xT = np.stack([xp.T.reshape(NKC, 128, 512), xs.T.reshape(NKC, 128, 512)], axis=0).transpose(2, 0, 1, 3)
        par = np.zeros((128, NPAR), np.float32)
        par[:, P_CV:P_CV + 32] = np.stack([fm(c_ctx), fm(f32(c)[s])], axis=2).reshape(128, 32)
        for l in range(NL):
            par[:, P_BA + l * 96:P_BA + (l + 1) * 96] = fm(f32(b_ada)[l])
            par[:, P_NMIX + l * 16:P_NMIX + (l + 1) * 16] = fm(f32(norm_mix)[l])
            par[:, P_NMLP + l * 16:P_NMLP + (l + 1) * 16] = fm(f32(norm_mlp)[l])
            for tap in range(3):
                par[:, P_CW + (l * 3 + tap) * 8:P_CW + (l * 3 + tap) * 8 + 8] = fm(f32(conv_w)[l, tap])
            par[:, P_SINK + l * 8:P_SINK + (l + 1) * 8] = f32(win_sink)[l][None, :]
            for off, arr in ((P_LQ1, lambda_q1), (P_LK1, lambda_k1), (P_LQ2, lambda_q2), (P_LK2, lambda_k2)):
                par[:, off + l * 64:off + (l + 1) * 64] = f32(arr)[l][None, :]
            par[:, P_SUB + l] = f32(diff_subln)[l]
        par[:, P_NFIN:P_NFIN + 16] = fm(norm_final)
        if j > 0:
            par[:, P_SELL + j - 1] = 1.0
        if j < 3:
            par[:, P_SELR + j + 1] = 1.0
        m = dict(shared)
        m["xT"] = np.ascontiguousarray(xT)
        m["par"] = par
        m["rope"] = _rope_tables(j)
        m["ckw"] = np.ascontiguousarray(f32(cache_win_k)[s].transpose(0, 3, 2, 1))
        m["cvw"] = np.ascontiguousarray(f32(cache_win_v)[s].reshape(NL, 512, 256))
        m["ckd"] = np.ascontiguousarray(f32(cache_diff_k)[s].transpose(0, 2, 3, 1))
        m["cvd"] = np.ascontiguousarray(f32(cache_diff_v)[s].reshape(NL, 512, 1024))
        in_maps.append(m)
    ncore = _DEBUG_CORES
    res = run_bass_kernel_spmd(nc, in_maps[:ncore], core_ids=list(range(ncore)))
    R = list(res.results) + [res.results[0]] * (8 - ncore)
    y_prompt = np.zeros((16, 256, D_MODEL), np.float32)
    y_sample = np.zeros((2, 2048, D_MODEL), np.float32)
    nwk = np.zeros((16, NL, 256, 2, 128), np.float32)
    nwv = np.zeros((16, NL, 256, 2, 128), np.float32)
    ndk = np.zeros((16, NL, 256, 8, 128), np.float32)
    ndv = np.zeros((16, NL, 256, 8, 128), np.float32)
    for core in range(8):
        s, j = core // 4, core % 4
        yT = np.asarray(R[core]["yT"], dtype=np.float32)
        yg = yT.transpose(1, 3, 2, 0).reshape(2, 512, D_MODEL)
        y_prompt[2 * core:2 * core + 2] = yg[0].reshape(2, 256, D_MODEL)
        y_sample[s, 512 * j:512 * (j + 1)] = yg[1]
        kwT = np.asarray(R[core]["kwT"], dtype=np.float32)
        nwk[2 * core:2 * core + 2] = kwT.transpose(3, 1, 2, 0).reshape(2, 256, NL, 2, 128).transpose(0, 2, 1, 3, 4)
        vw = np.asarray(R[core]["vw"], dtype=np.float32)
        nwv[2 * core:2 * core + 2] = vw.reshape(NL, 2, 256, 2, 128).transpose(1, 0, 2, 3, 4)
        kdT = np.asarray(R[core]["kdT"], dtype=np.float32)
        ndk[2 * core:2 * core + 2] = kdT.transpose(3, 1, 2, 0).reshape(2, 256, NL, 8, 128).transpose(0, 2, 1, 3, 4)
        vd = np.asarray(R[core]["vd"], dtype=np.float32)
        ndv[2 * core:2 * core + 2] = vd.reshape(NL, 2, 256, 8, 128).transpose(1, 0, 2, 3, 4)
    return (y_prompt, y_sample, nwk, nwv, ndk, ndv)
```

```python
from contextlib import ExitStack
import math
import numpy as np
import ml_dtypes
import concourse.bass as bass
import concourse.mybir as mybir
from concourse.bass_utils import run_bass_kernel_spmd

F32 = mybir.dt.float32
BF16 = mybir.dt.bfloat16
ALU = mybir.AluOpType
AF = mybir.ActivationFunctionType


class Prog:
    ENG = ('pe', 'act', 'dve', 'pool', 'sp')

    def __init__(self, nc):
        self.nc = nc
        self.stack = ExitStack()
        self.ops = {e: [] for e in self.ENG}
        self.sem = {}
        self.cnt = {}
        self.waited = {}
        self.last_w = {}
        self.readers = {}
        self.out_tokens = {}
        for e in self.ENG:
            self._mksem('E_' + e)

    def _mksem(self, name):
        if name not in self.sem:
            self.sem[name] = self.stack.enter_context(self.nc.semaphore(name))
            self.cnt[name] = 0
        return name

    def sb(self, name, shape, dt):
        return self.stack.enter_context(self.nc.sbuf_tensor(name, list(shape), dt))

    def ps(self, name, shape, dt=F32):
        return self.stack.enter_context(self.nc.psum_tensor(name, list(shape), dt))

    def _collect(self, eng, reads, writes):
        deps = {}

        def add(tok):
            if tok is None:
                return
            s, v, te = tok
            if te == 'pe' and eng == 'pe':
                return
            if deps.get(s, 0) < v:
                deps[s] = v
        for r in reads:
            add(self.last_w.get(r))
        for w in writes:
            add(self.last_w.get(w))
            for s, (v, te) in self.readers.get(w, {}).items():
                if te == eng and te != 'dma':
                    continue
                add((s, v, te))
        waits = []
        for s, v in deps.items():
            if self.waited.get((eng, s), 0) < v:
                self.waited[(eng, s)] = v
                waits.append((s, v))
        return waits

    def _record(self, tok, reads, writes):
        s, v, te = tok
        for r in reads:
            d = self.readers.setdefault(r, {})
            if d.get(s, (0, te))[0] <= v:
                d[s] = (v, te)
        for w in writes:
            self.last_w[w] = tok
            self.readers[w] = {}

    @staticmethod
    def _excl(reads, writes):
        pr = [r for r in reads if isinstance(r, str) and r.startswith('ps')]
        if not pr:
            return list(reads), list(writes)
        return [r for r in reads if r not in pr], list(writes) + [r for r in pr if r not in writes]

    def op(self, eng, fn, reads=(), writes=()):
        reads, writes = self._excl(reads, writes)
        waits = self._collect(eng, reads, writes)
        s = 'E_' + eng
        self.cnt[s] += 1
        self.ops[eng].append((waits, fn, (s, 1)))
        self._record((s, self.cnt[s], eng), reads, writes)

    def mm(self, fns, reads=(), writes=()):
        reads, writes = self._excl(reads, writes)
        waits = self._collect('pe', reads, writes)
        s = 'E_pe'
        self.cnt[s] += 1
        n = len(fns)
        for i, fn in enumerate(fns):
            self.ops['pe'].append((waits if i == 0 else [], fn, (s, 1) if i == n - 1 else None))
        self._record((s, self.cnt[s], 'pe'), reads, writes)

    def dma(self, q, fns, reads=(), writes=(), semkey=None, out=False, inc=16):
        if semkey is None:
            semkey = 'OUT' if out else 'D_' + str(writes[0])
        self._mksem(semkey)
        waits = self._collect(q, reads, writes)
        for i, fn in enumerate(fns):
            self.ops[q].append((waits if i == 0 else [], fn, (semkey, inc)))
        self.cnt[semkey] += inc * len(fns)
        tok = (semkey, self.cnt[semkey], 'dma')
        self._record(tok, reads, [] if out else writes)
        if out:
            self.out_tokens[q] = tok

    def finish(self):
        if 'OUT' in self.sem:
            v = self.cnt['OUT']
            self.ops['sp'].append(([('OUT', v)], None, None))
        nc = self.nc
        engs = {'pe': 'tensor', 'act': 'scalar', 'dve': 'vector', 'pool': 'gpsimd', 'sp': 'sync'}
        sem = self.sem
        with nc.Block() as block:
            for ename, bname in engs.items():
                oplist = self.ops[ename]

                def body(e, oplist=oplist):
                    for waits, fn, inc in oplist:
                        for s, v in waits:
                            e.wait_ge(sem[s], v)
                        if fn is None:
                            continue
                        ins = fn(e)
                        if inc is not None:
                            ins.then_inc(sem[inc[0]], inc[1])
                getattr(block, bname)(body)
        self.stack.close()


D_MODEL = 2048
NKC = 16
T = 512
NL = 2
KSUB = 8
NWB = 3
EPS = 1e-6
CB, CC, CX, WQ, WK, WV, DQ, DK, DV, GA, GB, GC = 0, 1024, 2048, 3072, 4096, 4352, 4608, 5632, 6656, 7680, 9728, 11776
IN_TOTAL = 13824
P_CV, P_BA, P_NMIX, P_NMLP, P_NFIN, P_CW, P_SINK, P_LQ1, P_LK1, P_LQ2, P_LK2, P_SUB, P_SELL, P_SELR, NPAR = \
    0, 32, 224, 256, 288, 304, 352, 368, 496, 624, 752, 880, 882, 886, 940
P_CV3, P_SELS = 890, 938
ALL8 = [[0, 1, 2, 3, 4, 5, 6, 7]]
NRW = 256
NRD = 2052
LAM_INIT = [0.8 - 0.6 * math.exp(-0.3 * l) for l in range(NL)]
GROUPS = [[0, 1, 2, 3], [4, 5, 6, 7]]
_GROUPS_OVERRIDE = []
_DEBUG_NOCC = []


def MM(out, lhsT, rhs, start, stop):
    return lambda e: e.matmul(out, lhsT=lhsT, rhs=rhs, start=start, stop=stop)


def ACT(out, in_, func, **kw):
    return lambda e: e.activation(out=out, in_=in_, func=func, **kw)


def TT(out, in0, in1, op):
    return lambda e: e.tensor_tensor(out=out, in0=in0, in1=in1, op=op)


def TS(out, in0, s1, s2, op0, op1):
    return lambda e: e.tensor_scalar(out=out, in0=in0, scalar1=s1, scalar2=s2, op0=op0, op1=op1)


def STT(out, in0, scalar, in1, op0, op1):
    return lambda e: e.scalar_tensor_tensor(out=out, in0=in0, scalar=scalar, in1=in1, op0=op0, op1=op1)


def TSA(out, in0, s1):
    return lambda e: e.tensor_scalar_add(out, in0, s1)


def TSM(out, in0, s1):
    return lambda e: e.tensor_scalar_mul(out, in0, s1)


def CP(out, in_):
    return lambda e: e.tensor_copy(out=out, in_=in_)


def RCP(out, in_):
    return lambda e: e.reciprocal(out=out, in_=in_)


def DMA(out, in_):
    return lambda e: e.dma_start(out=out, in_=in_)


class _Stop(Exception):
    pass


def build_program(debug_stop=None):
    nc = bass.Bass("TRN2", target_bir_lowering=False)

    def chk(name):
        if debug_stop is not None and name == debug_stop:
            raise _Stop()

    def din(name, shape, dt=F32):
        return nc.dram_tensor(name, list(shape), dt, kind="ExternalInput").ap()

    def dout(name, shape, dt=F32):
        return nc.dram_tensor(name, list(shape), dt, kind="ExternalOutput").ap()

    d_x = din("xT", [128, 2, NKC, T])
    d_wada = din("w_ada", [NL, D_MODEL, 1536])
    d_win = din("w_in", [NL, D_MODEL, IN_TOTAL])
    d_wbr = [din("w_branch_conv", [NL, 1024, D_MODEL]), din("w_branch_win", [NL, 1024, D_MODEL]),
             din("w_branch_diff", [NL, 1024, D_MODEL])]
    d_wout = din("w_out", [NL, D_MODEL, D_MODEL])
    d_w1 = din("w_mlp1", [NL, D_MODEL, 4 * D_MODEL])
    d_w2 = din("w_mlp2", [NL, 4 * D_MODEL, D_MODEL])
    d_par = din("par", [128, NPAR])
    d_rope = din("rope", [128, 4, T])
    d_cmask = din("cmask", [128, 2, T], BF16)
    d_perm = din("perm", [128, 2, 128], BF16)
    d_ckw = din("ckw", [NL, 128, 2, 512])
    d_cvw = din("cvw", [NL, 512, 256])
    d_ckd = din("ckd", [NL, 8, 128, 512])
    d_cvd = din("cvd", [NL, 512, 1024])
    o_y = dout("yT", [128, 2, NKC, T])
    o_kw = dout("kwT", [128, NL, 2, T])
    o_vw = dout("vw", [NL, T, 256])
    o_kd = dout("kdT", [128, NL, 8, T])
    o_vd = dout("vd", [NL, T, 1024])
    sendM = nc.dram_tensor("sendM", [128, 72], F32)
    gathM = nc.dram_tensor("gathM", [1024, 72], F32)
    sendW = [nc.dram_tensor(f"sendW{l}", [NRW, 512], BF16) for l in range(NL)]
    gathW = [nc.dram_tensor(f"gathW{l}", [4 * NRW, 512], BF16) for l in range(NL)]
    sendK = [nc.dram_tensor(f"sendK{l}", [1024, 512], BF16) for l in range(NL)]
    gathK = [nc.dram_tensor(f"gathK{l}", [4096, 512], BF16) for l in range(NL)]
    sendV = [nc.dram_tensor(f"sendV{l}", [1024, 512], BF16) for l in range(NL)]
    gathV = [nc.dram_tensor(f"gathV{l}", [4096, 512], BF16) for l in range(NL)]
    sendP = [nc.dram_tensor(f"sendP{l}", [8, 512], BF16) for l in range(NL)]
    gathP = [nc.dram_tensor(f"gathP{l}", [32, 512], BF16) for l in range(NL)]

    P = Prog(nc)
    xT = P.sb("xT_s", [128, 2, NKC, T], F32)
    hT = P.sb("hT", [128, NKC, T], BF16)
    wbufs = [P.sb(f"wb{i}", [128, KSUB, 512], BF16) for i in range(NWB)]
    arena = P.sb("arena", [128, 24576], BF16)
    pb = [P.sb("pb0", [128, 2, 258], F32), P.sb("pb1", [128, 1, 514], F32)]
    tmp = {k: P.sb(k, [128, 512], F32) for k in ("tA", "tB", "tC", "tD")}
    rstd = P.sb("rstd", [128, 512], F32)
    Eb = [P.sb(f"E{i}", [128, 2, 512], BF16) for i in range(3)]
    xb = [P.sb(f"xb{i}", [128, 512], BF16) for i in range(2)]
    sqb = P.sb("sqb", [128, 512], BF16)
    kvk = [P.sb(f"kvk{i}", [128, 1024], BF16) for i in range(3)]
    kvv = [P.sb(f"kvv{i}", [128, 8, 128], BF16) for i in range(3)]
    par = P.sb("par_s", [128, NPAR], F32)
    rope = P.sb("rope_s", [128, 4, T], F32)
    cmask = P.sb("cmask_s", [128, 2, T], BF16)
    perm = P.sb("perm_s", [128, 2, 128], BF16)
    ones = P.sb("ones", [128, 128], BF16)
    epst = P.sb("epst", [128, 1], F32)
    svec = P.sb("svec", [128, NKC, 3], BF16)
    svf = P.sb("svf", [128, NKC * 3], F32)
    mod = P.sb("mod", [128, NL, 2, 96], F32)
    AB = P.sb("AB", [128, NL, 2, 2, NKC], F32)
    es = P.sb("es", [128, NL * 8], F32)
    lamt = P.sb("lamt", [128, 16], F32)
    lprod = P.sb("lprod", [128, 4, 64], F32)
    phs = P.sb("phs", [128, 16], BF16)
    pha = P.sb("pha", [128, 4, 16], BF16)
    cbb = P.sb("cbb", [128, 8, 2], F32)
    fix = P.sb("fix", [128, 4, 8], F32)
    ps = P.ps("ps", [128, 8, 512], F32)

    def av(a, b, dt=None, pat=None, **kw):
        v = arena[:, a:b]
        if dt is not None:
            v = v.bitcast(dt)
        if pat is not None:
            v = v.rearrange(pat, **kw)
        return v
    qw = av(0, 4096, None, "p (h t) -> p h t", t=T)
    kw_ = av(4096, 5120, None, "p (h t) -> p h t", t=T)
    vw_ = av(5120, 6144, None, "p (c f) -> p c f", f=256)
    kcw = av(6144, 7168, None, "p (h t) -> p h t", t=T)
    vcw = av(7168, 8192, None, "p (c f) -> p c f", f=256)
    khalo = av(8192, 10240, None, "p (h r k) -> p h r k", h=2, r=8)
    vhalo = av(10240, 12288, None, "p (r f) -> p r f", f=256)
    qd = av(0, 4096, None, "p (h t) -> p h t", t=T)
    kd = av(4096, 8192, None, "p (h t) -> p h t", t=T)
    vd_ = av(8192, 12288, None, "p (c f) -> p c f", f=1024)
    merged = av(0, 8192, None, "p (c t) -> p c t", t=T)
    y_a = av(12288, 16384, None, "p (c t) -> p c t", t=T)
    y_b = av(16384, 20480, None, "p (c t) -> p c t", t=T)
    y_c = av(20480, 24576, None, "p (c t) -> p c t", t=T)
    cbS = av(16384, 20480, F32, "p (c t) -> p c t", t=T)
    ccS = av(20480, 24576, F32, "p (c t) -> p c t", t=T)
    act = av(0, 16384, None, "p (c t) -> p c t", t=T)
    R0a, R0b, R0c, YA, YB, YC = 'R0a', 'R0b', 'R0c', 'YA', 'YB', 'YC'
    H = [('h', kc) for kc in range(NKC)]

    def psk(b):
        return f'ps{b}'

    def bank(b, n=512):
        return ps[:, b, 0:n]

    def pcol(c):
        return par[:, c:c + 1]

    def emit():
        P.dma('sp', [DMA(par[:], d_par)], writes=['par'])
        P.dma('sp', [DMA(rope[:], d_rope)], writes=['rope'])
        P.dma('sp', [DMA(cmask[:], d_cmask), DMA(perm[:], d_perm)], writes=['cmask'])
        P.dma('sp', [DMA(xT[:, g, kc * 8:(kc + 1) * 8, :], d_x[:, g, kc * 8:(kc + 1) * 8, :]) for g in range(2) for kc in range(2)],
              writes=[('x', g, kc) for g in range(2) for kc in range(NKC)], semkey='D_x')
        P.op('dve', lambda e: e.memset(ones[:], 1.0), writes=['ones'])
        P.op('dve', lambda e: e.memset(epst[:], EPS), writes=['eps'])
        P.op('dve', lambda e: e.memset(pb[0][:], 0.0), writes=['pb0'])
        P.op('dve', lambda e: e.memset(pb[1][:], 0.0), writes=['pb1'])
        P.op('act', ACT(svf[:], par[:, P_CV3:P_CV3 + 48], AF.Silu), reads=['par'], writes=['svf'])
        P.op('dve', CP(svec[:].rearrange("p k v -> p (k v)"), svf[:]), reads=['svf'], writes=['svec'])
        P.op('act', ACT(es[:], par[:, P_SINK:P_SINK + 16], AF.Exp), reads=['par'], writes=['es'])
        P.op('dve', TT(lprod[:, 0:2, :], par[:, P_LQ1:P_LQ1 + 128].rearrange("p (l i) -> p l i", l=2),
                       par[:, P_LK1:P_LK1 + 128].rearrange("p (l i) -> p l i", l=2), ALU.mult), reads=['par'], writes=['lprodA'])
        P.op('dve', TT(lprod[:, 2:4, :], par[:, P_LQ2:P_LQ2 + 128].rearrange("p (l i) -> p l i", l=2),
                       par[:, P_LK2:P_LK2 + 128].rearrange("p (l i) -> p l i", l=2), ALU.mult), reads=['par'], writes=['lprodB'])
        P.op('dve', lambda e: e.reduce_sum(out=lamt[:, 0:4], in_=lprod[:], axis=mybir.AxisListType.X),
             reads=['lprodA', 'lprodB'], writes=['lamt'])
        P.op('act', ACT(lamt[:, 4:8], lamt[:, 0:4], AF.Exp), reads=['lamt'], writes=['lamt2'])
        for l in range(NL):
            P.op('dve', STT(lamt[:, 8 + l:9 + l], lamt[:, 6 + l:7 + l], -LAM_INIT[l], lamt[:, 4 + l:5 + l], ALU.add, ALU.subtract),
                 reads=['lamt2'], writes=[('nlam', l)])
            P.op('dve', TS(lamt[:, 10 + l:11 + l], par[:, P_SUB + l:P_SUB + l + 1], 1.0 - LAM_INIT[l], 0.0, ALU.mult, ALU.add),
                 reads=['par'], writes=[('gsub', l)])

        wstate = {'i': 0}

        def wload(src, ksub, ncol):
            i = wstate['i'] % NWB
            wstate['i'] += 1
            key = f'wb{i}'
            P.dma('pool', [DMA(wbufs[i][:, 0:ksub, 0:ncol], src.rearrange("(k p) c -> p k c", p=128))], writes=[key], semkey='D_' + key)
            return wbufs[i], key

        def fm_tile(W2d, col0, ncol, K, rhs_fn, rhs_reads, banks, evac, N=512):
            nkc = K // 128
            nf = ncol // 128
            for ks in range(nkc // KSUB):
                buf, key = wload(W2d[ks * KSUB * 128:(ks + 1) * KSUB * 128, col0:col0 + ncol], KSUB, ncol)
                fns = []
                for j in range(nf):
                    for kk in range(KSUB):
                        kc = ks * KSUB + kk
                        fns.append(MM(bank(banks[j], N), buf[:, kk, j * 128:(j + 1) * 128], rhs_fn(kc), kc == 0, kc == nkc - 1))
                P.mm(fns, reads=[key] + rhs_reads, writes=[psk(banks[j]) for j in range(nf)])
            for j in range(nf):
                evac(j, banks[j])

        chk('const')
        mall = av(0, 1152, F32, "p (r c) -> p r c", r=8)
        msend = av(2048, 2192, F32)
        for l in range(NL):
            P.op('dve', lambda e, l=l: e.memset(ps[:, l, 0:36], 0.0), writes=[psk(l)])
            for ct in range(3):
                for ks in range(2):
                    buf, key = wload(d_wada[l, ks * 1024:(ks + 1) * 1024, ct * 512:(ct + 1) * 512], KSUB, 512)
                    fns = []
                    for j in range(4):
                        fc = ct * 4 + j
                        for kk in range(KSUB):
                            kc = ks * KSUB + kk
                            fns.append(lambda e, o=ps[:, l, fc * 3:fc * 3 + 3], w=buf[:, kk, j * 128:(j + 1) * 128], r=svec[:, kc, :], sp=(kc == NKC - 1):
                                       e.matmul(o, lhsT=w, rhs=r, start=False, stop=sp, skip_group_check=True))
                    P.mm(fns, reads=[key, 'svec'], writes=[psk(l)])
            P.op('dve', CP(msend[:, l * 36:(l + 1) * 36], ps[:, l, 0:36]), reads=[psk(l)], writes=['msend'])
        P.dma('sp', [DMA(sendM.ap(), msend)], reads=['msend'], writes=['sendM'])
        P.dma('pool', [lambda e: e.collective_compute("AllGather", ALU.bypass, replica_groups=ALL8, ins=[sendM.ap()], outs=[gathM.ap()])],
              reads=['sendM'], writes=['gathM'], semkey='CCM', inc=1)
        P.dma('sp', [DMA(mall, gathM.ap().rearrange("(r p) c -> p r c", p=128))], reads=['gathM'], writes=['mall'])
        mv = mall.rearrange("p r (l f v) -> p r l f v", l=2, v=3)
        r8 = lambda ap: ap.rearrange("p (r f) -> p r f", r=8)
        for l in range(NL):
            bada = r8(par[:, P_BA + l * 96:P_BA + (l + 1) * 96])
            P.op('dve', TT(r8(mod[:, l, 0, :]), mv[:, :, l, :, 0], bada, ALU.add), reads=['mall', 'par'], writes=[('mod', l, 0)])
            t1 = r8(tmp['tA'][:, 0:96])
            P.op('dve', TSM(t1, mv[:, :, l, :, 1], pcol(P_SELS)), reads=['mall', 'par'], writes=['tA'])
            P.op('dve', STT(t1, mv[:, :, l, :, 2], pcol(P_SELS + 1), t1, ALU.mult, ALU.add), reads=['mall', 'par', 'tA'], writes=['tA'])
            P.op('dve', TT(r8(mod[:, l, 1, :]), t1, bada, ALU.add), reads=['tA', 'par'], writes=[('mod', l, 1)])
            for v in range(2):
                P.op('dve', STT(AB[:, l, v, 0, :], mod[:, l, v, 16:32], 1.0, par[:, P_NMIX + l * 16:P_NMIX + (l + 1) * 16], ALU.add, ALU.mult),
                     reads=[('mod', l, v), 'par'], writes=[('AB', l, v)])
                P.op('dve', STT(AB[:, l, v, 1, :], mod[:, l, v, 64:80], 1.0, par[:, P_NMLP + l * 16:P_NMLP + (l + 1) * 16], ALU.add, ALU.mult),
                     reads=[('mod', l, v), 'par'], writes=[('AB', l, v)])

        tmp_rot = {'i': 0}

        def nexttmp():
            k = ("tA", "tB")[tmp_rot['i'] % 2]
            tmp_rot['i'] += 1
            return tmp[k], k

        def norm_stats(g):
            for kc in range(NKC):
                P.op('dve', TT(hT[:, kc, :], xT[:, g, kc, :], xT[:, g, kc, :], ALU.mult), reads=[('x', g, kc)], writes=[('h', kc)])
            P.mm([MM(bank(7), ones[:], hT[:, kc, :], kc == 0, kc == NKC - 1) for kc in range(NKC)], reads=H + ['ones'], writes=[psk(7)])
            P.op('act', ACT(rstd[:], bank(7), AF.Ln, scale=1.0 / D_MODEL, bias=epst[:, 0:1]), reads=[psk(7), 'eps'], writes=['rstd'])
            P.op('act', ACT(rstd[:], rstd[:], AF.Exp, scale=-0.5), reads=['rstd'], writes=['rstd'])

        def norm_to_h(l, g, which):
            norm_stats(g)
            boff = 0 if which == 0 else 48
            for kc in range(NKC):
                t, tk = nexttmp()
                P.op('dve', STT(t[:], xT[:, g, kc, :], AB[:, l, g, which, kc:kc + 1], rstd[:], ALU.mult, ALU.mult),
                     reads=[('x', g, kc), 'rstd', ('AB', l, g)], writes=[tk])
                P.op('act', ACT(hT[:, kc, :], t[:], AF.Identity, bias=mod[:, l, g, boff + kc:boff + kc + 1]),
                     reads=[tk, ('mod', l, g)], writes=[('h', kc)])

        rot = {'xs': 0, 'xb': 0, 'E': 0, 'S': 0, 'it': 0, 'kv': 0}

        def rope_evac(b, dest, dkey, ci, pi):
            i = rot['xb'] % 2
            rot['xb'] += 1
            xsb = 4 + rot['xs'] % 4
            rot['xs'] += 1
            P.op('act', ACT(xb[i][:], bank(b), AF.Copy), reads=[psk(b)], writes=[('xb', i)])
            P.mm([MM(bank(xsb), perm[:, pi, :], xb[i][:], True, True)], reads=[('xb', i), 'cmask'], writes=[psk(xsb)])
            P.op('dve', TT(tmp['tC'][:], bank(b), rope[:, ci, :], ALU.mult), reads=[psk(b), 'rope'], writes=['tC'])
            P.op('dve', TT(tmp['tD'][:], bank(xsb), rope[:, ci + 1, :], ALU.mult), reads=[psk(xsb), 'rope'], writes=['tD'])
            P.op('dve', TT(dest, tmp['tC'][:], tmp['tD'][:], ALU.add), reads=['tC', 'tD'], writes=[dkey])

        def stage_out(b, n, dst, src_is_bank=True):
            t, tk = nexttmp()
            P.op('dve', CP(t[:, 0:n], bank(b, n)), reads=[psk(b)], writes=[tk])
            P.dma('sp', [DMA(dst, t[:, 0:n])], reads=[tk], writes=['OUT'], out=True)

        hrhs = lambda kc: hT[:, kc, :]

        def run_pass(l, g):
            Win = d_win[l]
            norm_to_h(l, g, 0)
            for half in range(2):
                def ev_q(j, b, half=half):
                    h = half * 4 + j
                    if g == 0:
                        P.op('act', ACT(qw[:, h, :], bank(b), AF.Copy), reads=[psk(b)], writes=[R0a])
                    else:
                        rope_evac(b, qw[:, h, :], R0a, 0, 0)
                fm_tile(Win, WQ + half * 512, 512, D_MODEL, hrhs, H, [0, 1, 2, 3], ev_q)
            chk(f'wq{l}{g}')
            for ks in range(2):
                buf, key = wload(Win[ks * 1024:(ks + 1) * 1024, WK:WK + 512], KSUB, 512)
                fns = []
                for j in range(2):
                    for kk in range(KSUB):
                        kc = ks * KSUB + kk
                        fns.append(MM(bank(j), buf[:, kk, j * 128:(j + 1) * 128], hT[:, kc, :], kc == 0, kc == NKC - 1))
                for tt in range(4):
                    for kk in range(KSUB):
                        kc = ks * KSUB + kk
                        fns.append(MM(bank(2 + tt, 256), hT[:, kc, tt * 128:(tt + 1) * 128], buf[:, kk, 256:512], kc == 0, kc == NKC - 1))
                P.mm(fns, reads=[key] + H, writes=[psk(b) for b in range(6)])
            chk(f'wkvmm{l}{g}')
            for tt in range(4):
                P.op('act', ACT(vw_[:, tt, :], bank(2 + tt, 256), AF.Copy), reads=[psk(2 + tt)], writes=[R0b])
                if g == 0:
                    stage_out(2 + tt, 256, o_vw[l, tt * 128:(tt + 1) * 128, :])
            chk(f'kwev{l}{g}')
            for j in range(2):
                if g == 0:
                    P.op('act', ACT(kw_[:, j, :], bank(j), AF.Copy), reads=[psk(j)], writes=[R0b])
                    stage_out(j, 512, o_kw[:, l, j, :])
                else:
                    rope_evac(j, kw_[:, j, :], R0b, 0, 0)
            if g == 1:
                sw = sendW[l].ap()
                kview = lambda r0: sw[r0:r0 + 64, :].rearrange("r (a c) -> (r a) c", c=128)
                vview = lambda r0: sw[r0:r0 + 64, :].rearrange("r (a c) -> (r a) c", c=256)
                fns = []
                for hk in range(2):
                    fns.append(DMA(kview(0)[hk * 128:(hk + 1) * 128, :], kw_[:, hk, 0:128]))
                    fns.append(DMA(kview(64)[hk * 128:(hk + 1) * 128, :], kw_[:, hk, 384:512]))
                fns.append(DMA(vview(128), vw_[:, 0, :]))
                fns.append(DMA(vview(192), vw_[:, 3, :]))
                P.dma('sp', fns, reads=[R0b], writes=[('sendW', l)])
                if _DEBUG_NOCC:
                    P.dma('pool', [DMA(gathW[l].ap()[0:NRW, :], sendW[l].ap())], reads=[('sendW', l)], writes=[('gathW', l)], semkey=f'CCW{l}')
                else:
                    P.dma('pool', [lambda e: e.collective_compute("AllGather", ALU.bypass, replica_groups=(_GROUPS_OVERRIDE or GROUPS),
                                                                  ins=[sendW[l].ap()], outs=[gathW[l].ap()])],
                          reads=[('sendW', l)], writes=[('gathW', l)], semkey=f'CCW{l}', inc=1)
            chk(f'winproj{l}{g}')
            pv = pb[g]
            ns, nt = (2, 256) if g == 0 else (1, 512)
            for half in range(2):
                def ev_cb(j, b):
                    P.op('act', ACT(cbS[:, j, :], bank(b), AF.Copy), reads=[psk(b)], writes=[YB])

                def ev_cc(j, b):
                    P.op('act', ACT(ccS[:, j, :], bank(b), AF.Copy), reads=[psk(b)], writes=[YC])

                def ev_cx(j, b, half=half):
                    i = half * 4 + j
                    cw = lambda tap: pcol(P_CW + (l * 3 + tap) * 8 + i)
                    v3 = lambda ap: ap.rearrange("p (s t) -> p s t", s=ns)
                    P.op('dve', TT(pv[:, :, 1:1 + nt], v3(bank(b)), v3(ccS[:, j, :]), ALU.mult), reads=[psk(b), YC], writes=[('pb', g)])
                    tC, tD = tmp['tC'][:], tmp['tD'][:]
                    P.op('dve', TSM(v3(tC), pv[:, :, 1:1 + nt], cw(1)), reads=[('pb', g), 'par'], writes=['tC'])
                    P.op('dve', STT(v3(tD), pv[:, :, 0:nt], cw(0), v3(tC), ALU.mult, ALU.add), reads=[('pb', g), 'tC', 'par'], writes=['tD'])
                    P.op('dve', STT(v3(tC), pv[:, :, 2:2 + nt], cw(2), v3(tD), ALU.mult, ALU.add), reads=[('pb', g), 'tD', 'par'], writes=['tC'])
                    P.op('dve', TT(y_a[:, i, :], tC, cbS[:, j, :], ALU.mult), reads=['tC', YB], writes=[YA])
                    if g == 1:
                        P.op('dve', CP(phs[:, i:i + 1], pv[:, 0, 1:2]), reads=[('pb', g)], writes=['phs'])
                        P.op('dve', CP(phs[:, 8 + i:9 + i], pv[:, 0, 512:513]), reads=[('pb', g)], writes=['phs'])
                        P.op('dve', CP(cbb[:, i, 0:1], cbS[:, j, 0:1]), reads=[YB], writes=['cbb'])
                        P.op('dve', CP(cbb[:, i, 1:2], cbS[:, j, 511:512]), reads=[YB], writes=['cbb'])
                fm_tile(Win, CB + half * 512, 512, D_MODEL, hrhs, H, [0, 1, 2, 3], ev_cb)
                fm_tile(Win, CC + half * 512, 512, D_MODEL, hrhs, H, [4, 5, 6, 7], ev_cc)
                fm_tile(Win, CX + half * 512, 512, D_MODEL, hrhs, H, [0, 1, 2, 3], ev_cx)
            chk(f'conv{l}{g}')
            sc_w = 128 ** -0.5
            if g == 1:
                gw = gathW[l].ap().rearrange("(r n) c -> r n c", r=4)
                P.dma('pool', [DMA(kcw[:, :, :], d_ckw[l]),
                               DMA(vcw[:, :, :], d_cvw[l].rearrange("(c p) f -> p c f", p=128))], writes=[R0b], semkey='D_cw')
                fns = []
                for part, r0 in ((0, 64), (1, 0)):
                    kv4 = gw[:, r0:r0 + 64, :].rearrange("r n (a c) -> r (n a) c", c=128)
                    for hk in range(2):
                        fns.append(DMA(khalo[:, hk, part * 4:(part + 1) * 4, :], kv4[:, hk * 128:(hk + 1) * 128, :].rearrange("r d c -> d r c")))
                for part, r0 in ((0, 192), (1, 128)):
                    vv4 = gw[:, r0:r0 + 64, :].rearrange("r n (a c) -> r (n a) c", c=256)
                    fns.append(DMA(vhalo[:, part * 4:(part + 1) * 4, :], vv4.rearrange("r t c -> t r c")))
                P.dma('sp', fns, reads=[('gathW', l)], writes=[R0c])

            def win_block(hk, qsl, ysl, chunks):
                it = rot['it']
                rot['it'] += 1
                ob, db = 2 + it % 2, 4 + it % 2
                rhs_q = qw[:, 4 * hk:4 * hk + 4, qsl]
                n = len(chunks)
                Es = {}

                def qk(ci):
                    k_ap, v_ap, m_ap, s_ap, rd = chunks[ci]
                    sb_ = rot['S'] % 2
                    rot['S'] += 1
                    ei = rot['E'] % 3
                    rot['E'] += 1
                    E = Eb[ei][:, 0, :]
                    Es[ci] = (E, ei)
                    P.mm([MM(bank(sb_), k_ap, rhs_q, True, True)], reads=[R0a] + rd, writes=[psk(sb_)])
                    P.op('act', ACT(E, bank(sb_), AF.Exp, scale=sc_w), reads=[psk(sb_)], writes=[('E', ei)])
                    if m_ap is not None:
                        if s_ap is None:
                            P.op('dve', TT(E, E, m_ap, ALU.mult), reads=[('E', ei), 'cmask'], writes=[('E', ei)])
                        else:
                            P.op('dve', STT(E, E, s_ap, m_ap, ALU.mult, ALU.mult), reads=[('E', ei), 'cmask', 'par'], writes=[('E', ei)])

                def pv(ci):
                    k_ap, v_ap, m_ap, s_ap, rd = chunks[ci]
                    E, ei = Es[ci]
                    P.mm([MM(bank(ob), v_ap, E, ci == 0, ci == n - 1), MM(bank(db), ones[:], E, ci == 0, ci == n - 1)],
                         reads=[('E', ei), 'ones'] + rd, writes=[psk(ob), psk(db)])
                qk(0)
                for ci in range(n):
                    if ci + 1 < n:
                        qk(ci + 1)
                    pv(ci)
                tC = tmp['tC'][:]
                for gg in range(4):
                    h = 4 * hk + gg
                    P.op('dve', TSA(tC[:, gg * 128:(gg + 1) * 128], ps[:, db, gg * 128:(gg + 1) * 128], es[:, l * 8 + h:l * 8 + h + 1]),
                         reads=[psk(db), 'es'], writes=['tC'])
                P.op('dve', RCP(tC, tC), reads=['tC'], writes=['tC'])
                P.op('dve', TT(y_b[:, 4 * hk:4 * hk + 4, ysl], bank(ob).rearrange("p (g q) -> p g q", g=4),
                               tC.rearrange("p (g q) -> p g q", g=4), ALU.mult), reads=[psk(ob), 'tC'], writes=[YB])

            mL, mR = cmask[:, 0, :], cmask[:, 1, :]
            if g == 0:
                for s in range(2):
                    for hk in range(2):
                        for qb in range(2):
                            sl = slice(s * 256 + qb * 128, s * 256 + qb * 128 + 128)
                            chunks = [(kw_[:, hk, s * 256 + c * 128:s * 256 + (c + 1) * 128], vw_[:, s * 2 + c, hk * 128:(hk + 1) * 128], None, None, [R0b])
                                      for c in range(2)]
                            win_block(hk, sl, sl, chunks)
            else:
                for hk in range(2):
                    for qb in range(4):
                        sl = slice(qb * 128, (qb + 1) * 128)
                        chunks = [(kcw[:, hk, c * 128:(c + 1) * 128], vcw[:, c, hk * 128:(hk + 1) * 128], None, None, [R0b]) for c in range(4)]
                        for cc in (qb - 1, qb, qb + 1):
                            if 0 <= cc <= 3:
                                m = mL if cc == qb - 1 else (mR if cc == qb + 1 else None)
                                chunks.append((kw_[:, hk, cc * 128:(cc + 1) * 128], vw_[:, cc, hk * 128:(hk + 1) * 128], m, None, [R0b]))
                        if qb == 0:
                            for r in range(4):
                                chunks.append((khalo[:, hk, r, :], vhalo[:, r, hk * 128:(hk + 1) * 128], mL, pcol(P_SELL + r), [R0c]))
                        if qb == 3:
                            for r in range(4):
                                chunks.append((khalo[:, hk, 4 + r, :], vhalo[:, 4 + r, hk * 128:(hk + 1) * 128], mR, pcol(P_SELR + r), [R0c]))
                        win_block(hk, sl, sl, chunks)
            chk(f'winattn{l}{g}')
            for half in range(2):
                def ev_dq(j, b, half=half):
                    h = half * 4 + j
                    if g == 0:
                        P.op('act', ACT(qd[:, h, :], bank(b), AF.Copy), reads=[psk(b)], writes=[R0a])
                    else:
                        rope_evac(b, qd[:, h, :], R0a, 2, 1)
                fm_tile(Win, DQ + half * 512, 512, D_MODEL, hrhs, H, [0, 1, 2, 3], ev_dq)
            for half in range(2):
                def ev_dk(j, b, half=half):
                    h = half * 4 + j
                    if g == 0:
                        P.op('act', ACT(kd[:, h, :], bank(b), AF.Copy), reads=[psk(b)], writes=[R0b])
                        stage_out(b, 512, o_kd[:, l, h, :])
                    else:
                        rope_evac(b, kd[:, h, :], R0b, 2, 1)
                fm_tile(Win, DK + half * 512, 512, D_MODEL, hrhs, H, [0, 1, 2, 3], ev_dk)
            for half in range(2):
                banks = [0, 1, 2, 3] if half == 0 else [4, 5, 6, 7]
                for ks in range(2):
                    buf, key = wload(Win[ks * 1024:(ks + 1) * 1024, DV + half * 512:DV + (half + 1) * 512], KSUB, 512)
                    fns = []
                    for tt in range(4):
                        for kk in range(KSUB):
                            kc = ks * KSUB + kk
                            fns.append(MM(bank(banks[tt]), hT[:, kc, tt * 128:(tt + 1) * 128], buf[:, kk, :], kc == 0, kc == NKC - 1))
                    P.mm(fns, reads=[key] + H, writes=[psk(b) for b in banks])
                for tt in range(4):
                    P.op('act', ACT(vd_[:, tt, half * 512:(half + 1) * 512], bank(banks[tt]), AF.Copy), reads=[psk(banks[tt])], writes=[R0c])
                    if g == 0:
                        stage_out(banks[tt], 512, o_vd[l, tt * 128:(tt + 1) * 128, half * 512:(half + 1) * 512])
            if g == 1:
                for nm, snd, gth, src_fn, rd in (
                        ('K', sendK[l], gathK[l], lambda: DMA(sendK[l].ap().rearrange("(h d) t -> d h t", d=128), kd[:, :, :]), [R0b]),
                        ('V', sendV[l], gathV[l], lambda: DMA(sendV[l].ap().rearrange("(t two) c -> t (two c)", two=2).rearrange("(c p) f -> p c f", p=128), vd_[:, :, :]), [R0c]),
                        ('P', sendP[l], gathP[l], lambda: DMA(sendP[l].ap()[0:4, :].rearrange("r (a c) -> (r a) c", c=16), phs[:]), ['phs'])):
                    P.dma('sp', [src_fn()], reads=rd, writes=[('send' + nm, l)])
                    if _DEBUG_NOCC:
                        P.dma('pool', [DMA(gth.ap()[0:snd.ap().shape[0], :], snd.ap())], reads=[('send' + nm, l)], writes=[('gath' + nm, l)], semkey=f'CC{nm}{l}')
                    else:
                        P.dma('pool', [lambda e, snd=snd, gth=gth: e.collective_compute("AllGather", ALU.bypass, replica_groups=(_GROUPS_OVERRIDE or GROUPS),
                                                                                        ins=[snd.ap()], outs=[gth.ap()])],
                              reads=[('send' + nm, l)], writes=[('gath' + nm, l)], semkey=f'CC{nm}{l}', inc=1)
            chk(f'diffproj{l}{g}')
            sc_d = 64 ** -0.5
            nlam = lamt[:, 8 + l:9 + l]
            gsub = lamt[:, 10 + l:11 + l]

            def diff_finish(h, tsl, n, t0, t1):
                tD = tmp['tD'][:, 0:n]
                P.op('dve', STT(tD, t1, nlam, t0, ALU.mult, ALU.add), reads=['tA', 'tB', ('nlam', l)], writes=['tD'])
                P.op('dve', TT(sqb[:, 0:n], tD, tD, ALU.mult), reads=['tD'], writes=['sqb'])
                P.mm([MM(bank(0, n), ones[:], sqb[:, 0:n], True, True)], reads=['sqb', 'ones'], writes=[psk(0)])
                tC = tmp['tC'][:, 0:n]
                P.op('act', ACT(tC, bank(0, n), AF.Ln, scale=1.0 / 128, bias=epst[:, 0:1]), reads=[psk(0), 'eps'], writes=['tC'])
                P.op('act', ACT(tC, tC, AF.Exp, scale=-0.5), reads=['tC'], writes=['tC'])
                P.op('dve', STT(y_c[:, h, tsl], tD, gsub, tC, ALU.mult, ALU.mult), reads=['tD', 'tC', ('gsub', l)], writes=[YC])

            if g == 0:
                for s in range(2):
                    for h in range(8):
                        it = rot['it']
                        rot['it'] += 1
                        ob, db = 4 + it % 2, 6 + it % 2
                        tsl = slice(s * 256, (s + 1) * 256)
                        Es = {}

                        def qk(c, s=s, h=h, tsl=tsl):
                            sb_ = 2 * (rot['S'] % 2)
                            rot['S'] += 1
                            ei = rot['E'] % 3
                            rot['E'] += 1
                            E = Eb[ei][:, :, 0:256]
                            Es[c] = (E, ei)
                            ksl = slice(s * 256 + c * 128, s * 256 + (c + 1) * 128)
                            P.mm([MM(ps[:, sb_, 0:256], kd[0:64, h, ksl], qd[0:64, h, tsl], True, True),
                                  MM(ps[:, sb_ + 1, 0:256], kd[64:128, h, ksl], qd[64:128, h, tsl], True, True)],
                                 reads=[R0a, R0b], writes=[psk(sb_), psk(sb_ + 1)])
                            P.op('act', ACT(E, ps[:, sb_:sb_ + 2, 0:256], AF.Exp, scale=sc_d), reads=[psk(sb_), psk(sb_ + 1)], writes=[('E', ei)])

                        def pv(c, s=s, h=h, ob=ob, db=db):
                            E, ei = Es[c]
                            P.mm([MM(bank(ob), vd_[:, s * 2 + c, h * 128:(h + 1) * 128], E, c == 0, c == 1),
                                  MM(bank(db), ones[:], E, c == 0, c == 1)], reads=[('E', ei), 'ones', R0c], writes=[psk(ob), psk(db)])
                        qk(0)
                        qk(1)
                        pv(0)
                        pv(1)
                        tA, tB = tmp['tA'][:], tmp['tB'][:]
                        P.op('dve', RCP(tB, bank(db)), reads=[psk(db)], writes=['tB'])
                        P.op('dve', TT(tA, bank(ob), tB, ALU.mult), reads=[psk(ob), 'tB'], writes=['tA'])
                        diff_finish(h, tsl, 256, tA[:, 0:256], tA[:, 256:512])
            else:
                gd = gathK[l].ap().rearrange("(r n) c -> n r c", r=4)
                for h in range(8):
                    pieces = []
                    for pc in range(3):
                        si = rot['kv'] % 3
                        rot['kv'] += 1
                        kb, vb = kvk[si], kvv[si]
                        if pc == 0:
                            P.dma('pool', [DMA(kb[:, 0:512], d_ckd[l, h]),
                                           DMA(vb[:, 0:4, :], d_cvd[l].rearrange("(c p) f -> p c f", p=128)[:, :, h * 128:(h + 1) * 128])],
                                  writes=[('kv', si)], semkey=f'D_kv{si}')
                            nch = 4
                        else:
                            r0 = (pc - 1) * 2
                            fns = [DMA(kb[:, :].rearrange("p (r t) -> p r t", r=2), gd[h * 128:(h + 1) * 128, r0:r0 + 2, :])]
                            for rr in range(2):
                                vsrc = gathV[l].ap()[(r0 + rr) * 1024:(r0 + rr + 1) * 1024, :] \
                                    .rearrange("(t two) c -> t (two c)", two=2).rearrange("(c p) f -> p c f", p=128)
                                fns.append(DMA(vb[:, rr * 4:(rr + 1) * 4, :], vsrc[:, :, h * 128:(h + 1) * 128]))
                            P.dma('sp', fns, reads=[('gathK', l), ('gathV', l)], writes=[('kv', si)], semkey=f'D_kv{si}')
                            nch = 8
                        pieces.append((si, nch))
                    tot = 20
                    clist = [(si, c) for si, nch in pieces for c in range(nch)]
                    Es = {}

                    def qk(ci, h=h):
                        si, c = clist[ci]
                        kb = kvk[si]
                        sb_ = 2 * (rot['S'] % 2)
                        rot['S'] += 1
                        ei = rot['E'] % 3
                        rot['E'] += 1
                        Es[ci] = ei
                        P.mm([MM(bank(sb_), kb[0:64, c * 128:(c + 1) * 128], qd[0:64, h, :], True, True),
                              MM(bank(sb_ + 1), kb[64:128, c * 128:(c + 1) * 128], qd[64:128, h, :], True, True)],
                             reads=[R0a, ('kv', si)], writes=[psk(sb_), psk(sb_ + 1)])
                        P.op('act', ACT(Eb[ei][:], ps[:, sb_:sb_ + 2, :], AF.Exp, scale=sc_d), reads=[psk(sb_), psk(sb_ + 1)], writes=[('E', ei)])

                    def pv(ci):
                        si, c = clist[ci]
                        vb = kvv[si]
                        ei = Es[ci]
                        st, sp_ = ci == 0, ci == tot - 1
                        P.mm([MM(bank(4), vb[:, c, :], Eb[ei][:, 0, :], st, sp_), MM(bank(5), vb[:, c, :], Eb[ei][:, 1, :], st, sp_),
                              MM(bank(6), ones[:], Eb[ei][:, 0, :], st, sp_), MM(bank(7), ones[:], Eb[ei][:, 1, :], st, sp_)],
                             reads=[('E', ei), 'ones', ('kv', si)], writes=[psk(4), psk(5), psk(6), psk(7)])
                    qk(0)
                    for ci in range(tot):
                        if ci + 1 < tot:
                            qk(ci + 1)
                        pv(ci)
                    tA, tB, tC = tmp['tA'][:], tmp['tB'][:], tmp['tC'][:]
                    P.op('dve', RCP(tC, bank(6)), reads=[psk(6)], writes=['tC'])
                    P.op('dve', TT(tA, bank(4), tC, ALU.mult), reads=[psk(4), 'tC'], writes=['tA'])
                    P.op('dve', RCP(tC, bank(7)), reads=[psk(7)], writes=['tC'])
                    P.op('dve', TT(tB, bank(5), tC, ALU.mult), reads=[psk(5), 'tC'], writes=['tB'])
                    diff_finish(h, slice(0, 512), 512, tA, tB)
                P.dma('sp', [DMA(pha[:, r, :], gathP[l].ap()[r * 8:r * 8 + 4, :].rearrange("r (a c) -> (r a) c", c=16)) for r in range(4)],
                      reads=[('gathP', l)], writes=['pha'])
                for side, selc, src0 in ((0, P_SELL, 8), (1, P_SELR, 0)):
                    acc = fix[:, side, :]
                    P.op('dve', TSM(acc, pha[:, 0, src0:src0 + 8], pcol(selc)), reads=['pha', 'par'], writes=[('fix', side)])
                    for r in range(1, 4):
                        P.op('dve', STT(acc, pha[:, r, src0:src0 + 8], pcol(selc + r), acc, ALU.mult, ALU.add), reads=['pha', 'par', ('fix', side)], writes=[('fix', side)])
                    tap = 0 if side == 0 else 2
                    cwv = par[:, P_CW + (l * 3 + tap) * 8:P_CW + (l * 3 + tap) * 8 + 8]
                    d2 = fix[:, 2 + side, :]
                    P.op('dve', TT(d2, acc, cwv, ALU.mult), reads=[('fix', side), 'par'], writes=[('fix', 2 + side)])
                    P.op('dve', TT(d2, d2, cbb[:, :, side], ALU.mult), reads=[('fix', 2 + side), 'cbb'], writes=[('fix', 2 + side)])
                    col = 0 if side == 0 else 511
                    P.op('dve', TT(y_a[:, :, col], y_a[:, :, col], d2, ALU.add), reads=[YA, ('fix', 2 + side)], writes=[YA])
            chk(f'diffattn{l}{g}')
            for br, (ysrc, ykey, gcol) in enumerate(((y_a, YA, GA), (y_b, YB, GB), (y_c, YC, GC))):
                for ct in range(4):
                    def ev_gate(j, b):
                        P.op('act', ACT(Eb[j // 2][:, j % 2, :], bank(b), AF.Sigmoid), reads=[psk(b)], writes=[('E', j // 2)])
                    fm_tile(Win, gcol + ct * 512, 512, D_MODEL, hrhs, H, [0, 1, 2, 3], ev_gate)

                    def ev_br(j, b, ct=ct, br=br):
                        fc = ct * 4 + j
                        gt = Eb[j // 2][:, j % 2, :]
                        if br == 0:
                            P.op('dve', TT(merged[:, fc, :], bank(b), gt, ALU.mult), reads=[psk(b), ('E', j // 2)], writes=[R0a, R0b])
                        else:
                            tC = tmp['tC'][:]
                            P.op('dve', TT(tC, bank(b), gt, ALU.mult), reads=[psk(b), ('E', j // 2)], writes=['tC'])
                            P.op('dve', TT(merged[:, fc, :], merged[:, fc, :], tC, ALU.add), reads=['tC', R0a, R0b], writes=[R0a, R0b])
                    fm_tile(d_wbr[br][l], ct * 512, 512, 1024, lambda kc, ysrc=ysrc: ysrc[:, kc, :], [ykey], [4, 5, 6, 7], ev_br)
            chk(f'branch{l}{g}')
            for ct in range(4):
                def ev_o(j, b, ct=ct):
                    fc = ct * 4 + j
                    P.op('dve', STT(xT[:, g, fc, :], bank(b), mod[:, l, g, 32 + fc:33 + fc], xT[:, g, fc, :], ALU.mult, ALU.add),
                         reads=[psk(b), ('mod', l, g), ('x', g, fc)], writes=[('x', g, fc)])
                fm_tile(d_wout[l], ct * 512, 512, D_MODEL, lambda kc: merged[:, kc, :], [R0a, R0b], [0, 1, 2, 3] if ct % 2 == 0 else [4, 5, 6, 7], ev_o)
            chk(f'wout{l}{g}')
            norm_to_h(l, g, 1)
            tile_i = 0
            for half in range(2):
                for ct in range(8):
                    def ev_m1(j, b, ct=ct):
                        t, tk = nexttmp()
                        P.op('act', ACT(t[:], bank(b), AF.Relu), reads=[psk(b)], writes=[tk])
                        P.op('dve', TT(act[:, ct * 4 + j, :], t[:], t[:], ALU.mult), reads=[tk], writes=[R0a, R0b, R0c, YA])
                    fm_tile(d_w1[l], half * 4096 + ct * 512, 512, D_MODEL, hrhs, H, [0, 1, 2, 3] if tile_i % 2 == 0 else [4, 5, 6, 7], ev_m1)
                    tile_i += 1
                for ct in range(4):
                    def ev_m2(j, b, ct=ct):
                        fc = ct * 4 + j
                        P.op('dve', STT(xT[:, g, fc, :], bank(b), mod[:, l, g, 80 + fc:81 + fc], xT[:, g, fc, :], ALU.mult, ALU.add),
                             reads=[psk(b), ('mod', l, g), ('x', g, fc)], writes=[('x', g, fc)])
                    fm_tile(d_w2[l][half * 4096:(half + 1) * 4096, :], ct * 512, 512, 4096, lambda kc: act[:, kc, :], [R0a, R0b, R0c, YA],
                            [0, 1, 2, 3] if tile_i % 2 == 0 else [4, 5, 6, 7], ev_m2)
                    tile_i += 1

        def final_norm(g):
            norm_stats(g)
            for kc in range(NKC):
                t, tk = nexttmp()
                P.op('dve', STT(t[:], xT[:, g, kc, :], par[:, P_NFIN + kc:P_NFIN + kc + 1], rstd[:], ALU.mult, ALU.mult),
                     reads=[('x', g, kc), 'rstd', 'par'], writes=[tk])
                P.dma('sp', [DMA(o_y[:, g, kc, :], t[:])], reads=[tk], writes=['OUT'], out=True)

        chk('p0')
        for l in range(NL):
            for g in range(2):
                run_pass(l, g)
                chk(f'pass{l}{g}')
                if l == NL - 1:
                    final_norm(g)

    try:
        emit()
    except _Stop:
        pass
    P.finish()
    return nc


_NC_CACHE = {}
_DEBUG_STOP = None
_DEBUG_CORES = 8


def _rope_tables(j):
    t = np.arange(512 * j, 512 * j + 512)
    row = (t // 64).astype(np.float32)
    col = (t % 64).astype(np.float32)
    out = np.zeros((128, 4, 512), np.float32)
    inv32 = (10000.0 ** (-np.arange(32, dtype=np.float32) / 32)).astype(np.float32)
    inv16 = (10000.0 ** (-np.arange(16, dtype=np.float32) / 16)).astype(np.float32)
    for d in range(128):
        blk = d // 32
        pos = row if blk < 2 else col
        ang = (pos * inv32[d % 32]).astype(np.float32)
        out[d, 0] = np.cos(ang)
        out[d, 1] = -np.sin(ang) if blk % 2 == 0 else np.sin(ang)
        e = d % 64
        blk = e // 16
        pos = row if blk < 2 else col
        ang = (pos * inv16[e % 16]).astype(np.float32)
        out[d, 2] = np.cos(ang)
        out[d, 3] = -np.sin(ang) if blk % 2 == 0 else np.sin(ang)
    return out


def _const_tables():
    perm = np.zeros((128, 2, 128), np.float32)
    for d in range(128):
        pw = d + 32 if (d % 64) < 32 else d - 32
        pd = d + 16 if (d % 32) < 16 else d - 16
        perm[pw, 0, d] = 1.0
        perm[pd, 1, d] = 1.0
    i = np.arange(128)[:, None]
    q = np.arange(128)[None, :]
    mL = (i >= q).astype(np.float32)
    mR = (i <= q).astype(np.float32)
    cmask = np.stack([np.tile(mL, (1, 4)), np.tile(mR, (1, 4))], axis=1)
    return perm.astype(ml_dtypes.bfloat16), cmask.astype(ml_dtypes.bfloat16)


def kernel(x_prompt, x_sample, cache_win_k, cache_win_v, cache_diff_k, cache_diff_v, c, c_ctx,
           w_ada, b_ada, norm_mix, norm_mlp, w_in, conv_w, win_sink,
           lambda_q1, lambda_k1, lambda_q2, lambda_k2, diff_subln,
           w_branch_conv, w_branch_win, w_branch_diff, w_out, w_mlp1, w_mlp2, norm_final):
    f32 = lambda a: np.ascontiguousarray(np.asarray(a, dtype=np.float32))
    x_prompt, x_sample = f32(x_prompt), f32(x_sample)
    if 'nc' not in _NC_CACHE:
        _NC_CACHE['nc'] = build_program(_DEBUG_STOP)
    nc = _NC_CACHE['nc']
    perm, cmask = _const_tables()
    w_ada = f32(w_ada)
    shared = {"w_in": f32(w_in), "w_branch_conv": f32(w_branch_conv), "w_branch_win": f32(w_branch_win),
              "w_branch_diff": f32(w_branch_diff), "w_out": f32(w_out), "w_mlp1": f32(w_mlp1), "w_mlp2": f32(w_mlp2),
              "perm": perm, "cmask": cmask}
    fm = lambda v: f32(v).reshape(-1, 128).T
    in_maps = []
    for core in range(8):
        s, j = core // 4, core % 4
        xp = x_prompt[2 * core:2 * core + 2].reshape(512, D_MODEL)
        xs = x_sample[s, 512 * j:512 * (j + 1)]
        xT = np.stack([xp.T.reshape(NKC, 128, 512), xs.T.reshape(NKC, 128, 512)], axis=0).transpose(2, 0, 1, 3)
        par = np.zeros((128, NPAR), np.float32)
        par[:, P_CV3:P_CV3 + 48] = np.stack([fm(c_ctx), fm(f32(c)[0]), fm(f32(c)[1])], axis=2).reshape(128, 48)
        par[:, P_SELS + s] = 1.0
        for l in range(NL):
            par[:, P_BA + l * 96:P_BA + (l + 1) * 96] = fm(f32(b_ada)[l])
            par[:, P_NMIX + l * 16:P_NMIX + (l + 1) * 16] = fm(f32(norm_mix)[l])
            par[:, P_NMLP + l * 16:P_NMLP + (l + 1) * 16] = fm(f32(norm_mlp)[l])
            for tap in range(3):
                par[:, P_CW + (l * 3 + tap) * 8:P_CW + (l * 3 + tap) * 8 + 8] = fm(f32(conv_w)[l, tap])
            par[:, P_SINK + l * 8:P_SINK + (l + 1) * 8] = f32(win_sink)[l][None, :]
            for off, arr in ((P_LQ1, lambda_q1), (P_LK1, lambda_k1), (P_LQ2, lambda_q2), (P_LK2, lambda_k2)):
                par[:, off + l * 64:off + (l + 1) * 64] = f32(arr)[l][None, :]
            par[:, P_SUB + l] = f32(diff_subln)[l]
        par[:, P_NFIN:P_NFIN + 16] = fm(norm_final)
        if j > 0:
            par[:, P_SELL + j - 1] = 1.0
        if j < 3:
            par[:, P_SELR + j + 1] = 1.0
        m = dict(shared)
        m["xT"] = np.ascontiguousarray(xT)
        m["w_ada"] = np.ascontiguousarray(w_ada[:, :, core * 1536:(core + 1) * 1536])
        m["par"] = par
        m["rope"] = _rope_tables(j)
        m["ckw"] = np.ascontiguousarray(f32(cache_win_k)[s].transpose(0, 3, 2, 1))
        m["cvw"] = np.ascontiguousarray(f32(cache_win_v)[s].reshape(NL, 512, 256))
        m["ckd"] = np.ascontiguousarray(f32(cache_diff_k)[s].transpose(0, 2, 3, 1))
        m["cvd"] = np.ascontiguousarray(f32(cache_diff_v)[s].reshape(NL, 512, 1024))
        in_maps.append(m)
    ncore = _DEBUG_CORES
    res = run_bass_kernel_spmd(nc, in_maps[:ncore], core_ids=list(range(ncore)))
    R = list(res.results) + [res.results[0]] * (8 - ncore)
    y_prompt = np.zeros((16, 256, D_MODEL), np.float32)
    y_sample = np.zeros((2, 2048, D_MODEL), np.float32)
    nwk = np.zeros((16, NL, 256, 2, 128), np.float32)
    nwv = np.zeros((16, NL, 256, 2, 128), np.float32)
    ndk = np.zeros((16, NL, 256, 8, 128), np.float32)
    ndv = np.zeros((16, NL, 256, 8, 128), np.float32)
    for core in range(8):
        s, j = core // 4, core % 4
        yT = np.asarray(R[core]["yT"], dtype=np.float32)
        yg = yT.transpose(1, 3, 2, 0).reshape(2, 512, D_MODEL)
        y_prompt[2 * core:2 * core + 2] = yg[0].reshape(2, 256, D_MODEL)
        y_sample[s, 512 * j:512 * (j + 1)] = yg[1]
        kwT = np.asarray(R[core]["kwT"], dtype=np.float32)
        nwk[2 * core:2 * core + 2] = kwT.transpose(3, 1, 2, 0).reshape(2, 256, NL, 2, 128).transpose(0, 2, 1, 3, 4)
        vw = np.asarray(R[core]["vw"], dtype=np.float32)
        nwv[2 * core:2 * core + 2] = vw.reshape(NL, 2, 256, 2, 128).transpose(1, 0, 2, 3, 4)
        kdT = np.asarray(R[core]["kdT"], dtype=np.float32)
        ndk[2 * core:2 * core + 2] = kdT.transpose(3, 1, 2, 0).reshape(2, 256, NL, 8, 128).transpose(0, 2, 1, 3, 4)
        vd = np.asarray(R[core]["vd"], dtype=np.float32)
        ndv[2 * core:2 * core + 2] = vd.reshape(NL, 2, 256, 8, 128).transpose(1, 0, 2, 3, 4)
    return (y_prompt, y_sample, nwk, nwv, ndk, ndv)
```
